# Optimizing a Trainium2 kernel written in Bass

```python
import math
import jax, jax.numpy as jnp
from jax import lax
import numpy as np

D_MODEL = 1024
BATCH = 16
SEQ = 2048
DEPTH = 2

MEM_LEN = 256
A_HEADS = 4
A_DH = 64
A_DV = 2 * A_DH
B_GROUPS = 4
B_DG = 64
B_WIDTH = B_GROUPS * B_DG
B_CHUNK = 128
C_HEADS = 4
C_DH = 64
C_BLOCK = 256
C_TOPK = 3
C_QCHUNK = 64
A_Q_COLS = A_HEADS * 2 * A_DH
A_K_COLS = A_HEADS * 2 * A_DH
A_V_COLS = A_HEADS * A_DV
B_COLS = 2 * B_WIDTH
C_COLS = C_HEADS * C_DH
MIX_WIDTH = A_HEADS * A_DV + B_WIDTH + C_HEADS * C_DH
IN_COLS = A_Q_COLS + A_K_COLS + A_V_COLS + B_COLS + 3 * C_COLS
XA_HEADS = 4
XA_DH = D_MODEL // XA_HEADS
D_FF = 2816
ROPE_THETA = 10000.0
ATTN_QBLOCK = 128
LN_EPS = 1e-5
DEEPNORM_ALPHA = (2.0 * DEPTH) ** 0.25
DEEPNORM_BETA = (8.0 * DEPTH) ** -0.25

kernel_name = "hymba_style_diff_sgu_moba_macaron_deepnorm"


def layer_norm(x, g, b):
    xf = x.astype(jnp.float32)
    mu = jnp.mean(xf, axis=-1, keepdims=True)
    var = jnp.mean(jnp.square(xf - mu), axis=-1, keepdims=True)
    y = (xf - mu) * lax.rsqrt(var + LN_EPS)
    return (y * g.astype(jnp.float32) + b.astype(jnp.float32)).astype(x.dtype)


def rms_norm(x, g):
    xf = x.astype(jnp.float32)
    y = xf * lax.rsqrt(jnp.mean(jnp.square(xf), axis=-1, keepdims=True) + LN_EPS)
    return (y * g.astype(jnp.float32)).astype(x.dtype)


def swiglu(x, w_gate, w_up, w_down):
    return (jax.nn.silu(x @ w_gate) * (x @ w_up)) @ w_down


def rope_tables(positions, dim):
    inv_freq = 1.0 / (ROPE_THETA ** (jnp.arange(0, dim, 2, dtype=jnp.float32) / dim))
    ang = positions.astype(jnp.float32)[..., None] * inv_freq
    return jnp.cos(ang)[:, :, None, :], jnp.sin(ang)[:, :, None, :]


def apply_rope(x, cos, sin):
    xf = x.astype(jnp.float32)
    x1, x2 = jnp.split(xf, 2, axis=-1)
    return jnp.concatenate([x1 * cos - x2 * sin, x2 * cos + x1 * sin], axis=-1).astype(x.dtype)


def diff_attention(q1, q2, k1, k2, v, lam):
    Bsz, S, H, d = q1.shape
    nqb = S // ATTN_QBLOCK
    scale = d ** -0.5
    qs = jnp.stack([q1, q2], axis=0).reshape(2, Bsz, nqb, ATTN_QBLOCK, H, d)
    qs = qs.transpose(2, 0, 1, 3, 4, 5)
    ks = jnp.stack([k1, k2], axis=0)
    kpos = jnp.arange(S)

    def block(args):
        i, qb = args
        qpos = i * ATTN_QBLOCK + jnp.arange(ATTN_QBLOCK)
        mask = kpos[None, :] <= qpos[:, None]
        s = jnp.einsum('mbqhd,mbkhd->mbhqk', qb, ks).astype(jnp.float32) * scale
        p = jax.nn.softmax(jnp.where(mask, s, -jnp.inf), axis=-1)
        w = p[0] - lam * p[1]
        return jnp.einsum('bhqk,bkhe->bqhe', w.astype(v.dtype), v)

    o = lax.map(block, (jnp.arange(nqb), qs))
    return o.transpose(1, 0, 2, 3, 4).reshape(Bsz, S, H, v.shape[-1])


def spatial_gating(u, v, ln_g, ln_b, w_s, b_s):
    Bsz, S, _ = v.shape
    nc = S // B_CHUNK
    v = layer_norm(v, ln_g, ln_b)
    vc = v.reshape(Bsz, nc, B_CHUNK, B_GROUPS, B_DG)
    w = jnp.tril(w_s)
    mix = jnp.einsum('gts,bnsgc->bntgc', w, vc) + b_s.T[None, None, :, :, None]
    return u * mix.reshape(Bsz, S, B_WIDTH)


def moba_attention(q, k, v):
    Bsz, S, H, d = q.shape
    S_pad = -(-S // C_BLOCK) * C_BLOCK
    pad = S_pad - S
    padw = ((0, 0), (0, pad), (0, 0), (0, 0))
    q, k, v = jnp.pad(q, padw), jnp.pad(k, padw), jnp.pad(v, padw)
    nB = S_pad // C_BLOCK
    topk = min(C_TOPK, nB)
    scale = d ** -0.5

    kb = k.reshape(Bsz, nB, C_BLOCK, H, d)
    vb = v.reshape(Bsz, nB, C_BLOCK, H, d)
    kbar = jnp.mean(kb.astype(jnp.float32), axis=2)
    gate = jnp.einsum('bshd,bnhd->bshn', q.astype(jnp.float32), kbar)
    qblk = jnp.arange(S_pad) // C_BLOCK
    past = jnp.arange(nB)[None, :] < qblk[:, None]
    gate = jnp.where(past[None, :, None, :], gate, -jnp.inf)
    _, sel = lax.top_k(gate, topk)
    valid = sel < qblk[None, :, None, None]

    kbt = kb.transpose(0, 3, 1, 2, 4)
    vbt = vb.transpose(0, 3, 1, 2, 4)
    b_i = jnp.arange(Bsz)[:, None, None, None]
    h_i = jnp.arange(H)[None, None, :, None]
    nqc = S_pad // C_QCHUNK

    def chunk(c):
        start = c * C_QCHUNK
        qc = lax.dynamic_slice_in_dim(q, start, C_QCHUNK, axis=1)
        selc = lax.dynamic_slice_in_dim(sel, start, C_QCHUNK, axis=1)
        validc = lax.dynamic_slice_in_dim(valid, start, C_QCHUNK, axis=1)
        blk_start = (start // C_BLOCK) * C_BLOCK
        k_own = lax.dynamic_slice_in_dim(k, blk_start, C_BLOCK, axis=1)
        v_own = lax.dynamic_slice_in_dim(v, blk_start, C_BLOCK, axis=1)
        k_sel = kbt[b_i, h_i, selc]
        v_sel = vbt[b_i, h_i, selc]
        s_sel = jnp.einsum('bqhd,bqhjtd->bhqjt', qc, k_sel).astype(jnp.float32) * scale
        s_sel = jnp.where(validc.transpose(0, 2, 1, 3)[..., None], s_sel, -jnp.inf)
        s_own = jnp.einsum('bqhd,bthd->bhqt', qc, k_own).astype(jnp.float32) * scale
        qpos = start + jnp.arange(C_QCHUNK)
        kpos = blk_start + jnp.arange(C_BLOCK)
        s_own = jnp.where(kpos[None, :] <= qpos[:, None], s_own, -jnp.inf)
        s = jnp.concatenate([s_sel.reshape(Bsz, H, C_QCHUNK, topk * C_BLOCK), s_own], axis=-1)
        p = jax.nn.softmax(s, axis=-1).astype(v.dtype)
        p_sel = p[..., :topk * C_BLOCK].reshape(Bsz, H, C_QCHUNK, topk, C_BLOCK)
        p_own = p[..., topk * C_BLOCK:]
        return (jnp.einsum('bhqjt,bqhjtd->bqhd', p_sel, v_sel)
                + jnp.einsum('bhqt,bthd->bqhd', p_own, v_own))

    o = lax.map(chunk, jnp.arange(nqc))
    o = o.transpose(1, 0, 2, 3, 4).reshape(Bsz, S_pad, H, d)
    return o[:, :S]


def hybrid_mixer(x, w_in, lq1, lk1, lq2, lk2, subln_g, sgu_ln_g, sgu_ln_b, sgu_w, sgu_b,
                 w_out, cos_a, sin_a, cos_c, sin_c, lam_init):
    Bsz, S, _ = x.shape
    h = x @ w_in
    cuts = np.cumsum([A_Q_COLS, A_K_COLS, A_V_COLS, B_WIDTH, B_WIDTH, C_COLS, C_COLS]).tolist()
    h_qa, h_ka, h_va, h_u, h_v, h_qc, h_kc, h_vc = jnp.split(h, cuts, axis=-1)

    qa = h_qa.reshape(Bsz, S, A_HEADS, 2, A_DH)
    ka = h_ka.reshape(Bsz, S, A_HEADS, 2, A_DH)
    q1 = apply_rope(qa[..., 0, :], cos_a, sin_a)
    q2 = apply_rope(qa[..., 1, :], cos_a, sin_a)
    k1 = apply_rope(ka[..., 0, :], cos_a, sin_a)
    k2 = apply_rope(ka[..., 1, :], cos_a, sin_a)
    va = h_va.reshape(Bsz, S, A_HEADS, A_DV)
    f32 = jnp.float32
    lam = (jnp.exp(jnp.sum(lq1.astype(f32) * lk1.astype(f32)))
           - jnp.exp(jnp.sum(lq2.astype(f32) * lk2.astype(f32))) + lam_init)
    oa = diff_attention(q1, q2, k1, k2, va, lam)
    oa = (rms_norm(oa, subln_g) * (1.0 - lam_init)).reshape(Bsz, S, A_HEADS * A_DV)

    ob = spatial_gating(jax.nn.gelu(h_u, approximate=False), jax.nn.gelu(h_v, approximate=False),
                        sgu_ln_g, sgu_ln_b, sgu_w, sgu_b)

    qc = apply_rope(h_qc.reshape(Bsz, S, C_HEADS, C_DH), cos_c, sin_c)
    kc = apply_rope(h_kc.reshape(Bsz, S, C_HEADS, C_DH), cos_c, sin_c)
    vc = h_vc.reshape(Bsz, S, C_HEADS, C_DH)
    oc = moba_attention(qc, kc, vc).reshape(Bsz, S, C_HEADS * C_DH)

    return jnp.concatenate([oa, ob, oc], axis=-1) @ w_out


def memory_cross_attention(x, mem, wq, wk, wv, wo):
    Bsz, S, _ = x.shape
    M = mem.shape[1]
    q = (x @ wq).reshape(Bsz, S, XA_HEADS, XA_DH)
    k = (mem @ wk).reshape(Bsz, M, XA_HEADS, XA_DH)
    v = (mem @ wv).reshape(Bsz, M, XA_HEADS, XA_DH)
    s = jnp.einsum('bshd,bmhd->bhsm', q, k).astype(jnp.float32) * (XA_DH ** -0.5)
    p = jax.nn.softmax(s, axis=-1).astype(v.dtype)
    o = jnp.einsum('bhsm,bmhd->bshd', p, v).reshape(Bsz, S, XA_HEADS * XA_DH)
    return o @ wo


def setup_inputs(seed: int = 0) -> dict:
    key = jax.random.key(seed)
    ks = iter(jax.random.split(key, 40))
    L, D, F = DEPTH, D_MODEL, D_FF

    def nrm(shape, scale):
        return jax.random.normal(next(ks), shape, jnp.float32) * scale

    def gain(shape):
        return 1.0 + nrm(shape, 0.02)

    x = nrm((BATCH, SEQ, D), 1.0)
    mem = nrm((BATCH, MEM_LEN, D), 1.0)
    offsets = jax.random.randint(next(ks), (BATCH, 1), 0, 4096, dtype=jnp.int32)
    positions = (offsets + jnp.arange(SEQ, dtype=jnp.int32)[None, :]).astype(jnp.int32)
    return {
        "x": x, "mem": mem, "positions": positions,
        "ffn1_w_gate": nrm((L, D, F), D ** -0.5),
        "ffn1_w_up": nrm((L, D, F), D ** -0.5),
        "ffn1_w_down": nrm((L, F, D), F ** -0.5 * DEEPNORM_BETA),
        "ln1_g": gain((L, D)), "ln1_b": nrm((L, D), 0.02),
        "mix_w_in": nrm((L, D, IN_COLS), D ** -0.5),
        "diff_lq1": nrm((L, A_DH), 0.1), "diff_lk1": nrm((L, A_DH), 0.1),
        "diff_lq2": nrm((L, A_DH), 0.1), "diff_lk2": nrm((L, A_DH), 0.1),
        "diff_subln_g": gain((L, A_DV)),
        "sgu_ln_g": gain((L, B_WIDTH)), "sgu_ln_b": nrm((L, B_WIDTH), 0.02),
        "sgu_w": nrm((L, B_GROUPS, B_CHUNK, B_CHUNK), B_CHUNK ** -0.5),
        "sgu_b": gain((L, B_GROUPS, B_CHUNK)),
        "mix_w_out": nrm((L, MIX_WIDTH, D), MIX_WIDTH ** -0.5 * DEEPNORM_BETA),
        "ln2_g": gain((L, D)), "ln2_b": nrm((L, D), 0.02),
        "xa_wq": nrm((L, D, XA_HEADS * XA_DH), D ** -0.5),
        "xa_wk": nrm((L, D, XA_HEADS * XA_DH), D ** -0.5),
        "xa_wv": nrm((L, D, XA_HEADS * XA_DH), D ** -0.5),
        "xa_wo": nrm((L, XA_HEADS * XA_DH, D), (XA_HEADS * XA_DH) ** -0.5 * DEEPNORM_BETA),
        "ln3_g": gain((L, D)), "ln3_b": nrm((L, D), 0.02),
        "ffn2_w_gate": nrm((L, D, F), D ** -0.5),
        "ffn2_w_up": nrm((L, D, F), D ** -0.5),
        "ffn2_w_down": nrm((L, F, D), F ** -0.5 * DEEPNORM_BETA),
        "ln4_g": gain((L, D)), "ln4_b": nrm((L, D), 0.02),
    }


def reference(x, mem, positions, ffn1_w_gate, ffn1_w_up, ffn1_w_down, ln1_g, ln1_b,
              mix_w_in, diff_lq1, diff_lk1, diff_lq2, diff_lk2, diff_subln_g,
              sgu_ln_g, sgu_ln_b, sgu_w, sgu_b, mix_w_out, ln2_g, ln2_b,
              xa_wq, xa_wk, xa_wv, xa_wo, ln3_g, ln3_b,
              ffn2_w_gate, ffn2_w_up, ffn2_w_down, ln4_g, ln4_b):
    cos_a, sin_a = rope_tables(positions, A_DH)
    cos_c, sin_c = rope_tables(positions, C_DH)
    for l in range(DEPTH):
        lam_init = 0.8 - 0.6 * math.exp(-0.3 * l)
        x = layer_norm(DEEPNORM_ALPHA * x + 0.5 * swiglu(x, ffn1_w_gate[l], ffn1_w_up[l], ffn1_w_down[l]),
                       ln1_g[l], ln1_b[l])
        mix = hybrid_mixer(x, mix_w_in[l], diff_lq1[l], diff_lk1[l], diff_lq2[l], diff_lk2[l],
                           diff_subln_g[l], sgu_ln_g[l], sgu_ln_b[l], sgu_w[l], sgu_b[l],
                           mix_w_out[l], cos_a, sin_a, cos_c, sin_c, lam_init)
        x = layer_norm(DEEPNORM_ALPHA * x + mix, ln2_g[l], ln2_b[l])
        xa = memory_cross_attention(x, mem, xa_wq[l], xa_wk[l], xa_wv[l], xa_wo[l])
        x = layer_norm(DEEPNORM_ALPHA * x + xa, ln3_g[l], ln3_b[l])
        x = layer_norm(DEEPNORM_ALPHA * x + 0.5 * swiglu(x, ffn2_w_gate[l], ffn2_w_up[l], ffn2_w_down[l]),
                       ln4_g[l], ln4_b[l])
    return x
```

```python
import math
import numpy as np
import concourse.bass as bass
import concourse.mybir as mybir
from concourse.bass_utils import run_bass_kernel_spmd

F32 = mybir.dt.float32
BF16 = mybir.dt.bfloat16
I32 = mybir.dt.int32
AF = mybir.ActivationFunctionType
ALU = mybir.AluOpType
AX = mybir.AxisListType

D = 1024
SEQ = 2048
BATCH = 16
DEPTH = 2
NCORES = 8
SPC = BATCH // NCORES
MEM = 256
DFF = 2816
NJ = DFF // 128
NB = 256
NBLK = SEQ // NB
ALPHA = (2.0 * DEPTH) ** 0.25
EPS = 1e-5
BIG = 30000.0
JG = 4
SLOT = 1024
NSLOT = 13


def _colpanel(W, c0, width):
    K = W.shape[0] // 128
    return W[:, c0:c0 + width].reshape(K, 128, width).transpose(1, 0, 2).reshape(128, K * width)


def _swap_cols(base):
    return list(range(base + 32, base + 64)) + list(range(base, base + 32))


class Layout:
    def __init__(self):
        self.off = {}
        self.tot = 0

    def add(self, name, size):
        self.off[name] = (self.tot, size)
        self.tot += size


def make_layout():
    L = Layout()
    for f in (1, 2):
        for j in range(NJ):
            L.add(f"g{f}_{j}", 1024)
            L.add(f"u{f}_{j}", 1024)
            L.add(f"d{f}_{j}", 1024)
    for c in range(4):
        L.add(f"ka_{c}", 1024)
        L.add(f"kas_{c}", 1024)
    for c in range(2):
        L.add(f"kc_{c}", 1024)
        L.add(f"kcs_{c}", 1024)
    for k in range(8):
        L.add(f"wv_{k}", 768)
    for c in range(4):
        L.add(f"qa_{c}", 1024)
        L.add(f"qas_{c}", 1024)
    for c in range(2):
        L.add(f"qc_{c}", 1024)
        L.add(f"qcs_{c}", 1024)
    for g in range(4):
        L.add(f"u_{g}", 512)
    for h in range(2):
        L.add(f"vs_{h}", 1024)
    for c in range(10):
        L.add(f"wo_{c}", 1024)
    for m in range(8):
        L.add(f"xq_{m}", 1024)
        L.add(f"xk_{m}", 1024)
    for k in range(8):
        L.add(f"xv_{k}", 1024)
    for c in range(8):
        L.add(f"xo_{c}", 1024)
    L.add("wst", 512)
    return L


LAY = make_layout()


def pack_layer(inp, l):
    out = np.zeros((128, LAY.tot), np.float32)

    def put(name, arr):
        o, s = LAY.off[name]
        assert arr.shape == (128, s), (name, arr.shape, s)
        out[:, o:o + s] = arr

    for f in (1, 2):
        wg, wu, wd = inp[f"ffn{f}_w_gate"][l], inp[f"ffn{f}_w_up"][l], inp[f"ffn{f}_w_down"][l]
        for j in range(NJ):
            put(f"g{f}_{j}", _colpanel(wg, j * 128, 128))
            put(f"u{f}_{j}", _colpanel(wu, j * 128, 128))
            put(f"d{f}_{j}", wd[j * 128:(j + 1) * 128, :])
    w_in = inp["mix_w_in"][l]
    swp = np.arange(w_in.shape[1])
    for base in list(range(0, 1024, 64)) + list(range(2048, 2560, 64)):
        swp[base:base + 64] = _swap_cols(base)
    w_sw = w_in[:, swp]
    for c in range(4):
        put(f"qa_{c}", _colpanel(w_in, c * 128, 128))
        put(f"qas_{c}", _colpanel(w_sw, c * 128, 128))
        put(f"ka_{c}", _colpanel(w_in, 512 + c * 128, 128))
        put(f"kas_{c}", _colpanel(w_sw, 512 + c * 128, 128))
    for c in range(2):
        put(f"qc_{c}", _colpanel(w_in, 2048 + c * 128, 128))
        put(f"qcs_{c}", _colpanel(w_sw, 2048 + c * 128, 128))
        put(f"kc_{c}", _colpanel(w_in, 2304 + c * 128, 128))
        put(f"kcs_{c}", _colpanel(w_sw, 2304 + c * 128, 128))
    wv = np.concatenate([w_in[:, 1024:1536], w_in[:, 2560:2816]], axis=1)
    for k in range(8):
        put(f"wv_{k}", wv[k * 128:(k + 1) * 128, :])
    for g in range(4):
        put(f"u_{g}", _colpanel(w_in, 1536 + g * 64, 64))
    wvs = w_in[:, 1792:2048]
    for h in range(2):
        put(f"vs_{h}", wvs[h * 512:(h + 1) * 512].reshape(4, 128, 256).transpose(1, 0, 2).reshape(128, 1024))
    w_out = inp["mix_w_out"][l]
    for c in range(4):
        put(f"wo_{c}", w_out[c * 128:(c + 1) * 128, :])
    for g in range(4):
        t = np.zeros((128, 1024), np.float32)
        t[:64] = w_out[512 + g * 64:512 + (g + 1) * 64, :]
        put(f"wo_{4 + g}", t)
    for hp in range(2):
        put(f"wo_{8 + hp}", w_out[768 + hp * 128:768 + (hp + 1) * 128, :])
    for m in range(8):
        put(f"xq_{m}", _colpanel(inp["xa_wq"][l], m * 128, 128))
        put(f"xk_{m}", _colpanel(inp["xa_wk"][l], m * 128, 128))
    for k in range(8):
        put(f"xv_{k}", inp["xa_wv"][l][k * 128:(k + 1) * 128, :])
    for c in range(8):
        put(f"xo_{c}", inp["xa_wo"][l][c * 128:(c + 1) * 128, :])
    put("wst", np.concatenate([inp["sgu_w"][l][g].T for g in range(4)], axis=1))
    return out


PCOLS = {}
_pc = 0
for _l in range(DEPTH):
    for _n in ("ln1_g", "ln1_b", "ln2_g", "ln2_b", "ln3_g", "ln3_b", "ln4_g", "ln4_b"):
        PCOLS[(_l, _n)] = _pc
        _pc += 8
    PCOLS[(_l, "subln")] = _pc
    _pc += 1
PCOLS["invf"] = _pc
_pc += 1
PCOLS["sign"] = _pc
_pc += 1
NPAR = _pc


def pack_params(inp):
    P = np.zeros((128, NPAR), np.float32)
    for l in range(DEPTH):
        for n in ("ln1_g", "ln1_b", "ln2_g", "ln2_b", "ln3_g", "ln3_b", "ln4_g", "ln4_b"):
            P[:, PCOLS[(l, n)]:PCOLS[(l, n)] + 8] = inp[n][l].reshape(8, 128).T
        P[:, PCOLS[(l, "subln")]] = inp["diff_subln_g"][l]
    invf = (np.float32(1.0) / (np.float32(10000.0) ** (np.arange(0, 64, 2, dtype=np.float32) / np.float32(64)))).astype(np.float32)
    P[:, PCOLS["invf"]] = np.tile(invf, 4)
    P[:, PCOLS["sign"]] = np.tile(np.concatenate([np.ones(32, np.float32), -np.ones(32, np.float32)]), 2)
    return P


def pack_bcast(inp):
    out = np.zeros((DEPTH, 128, 1280), np.float32)
    for l in range(DEPTH):
        row = np.concatenate([inp["sgu_ln_g"][l], inp["sgu_ln_b"][l], inp["diff_lq1"][l], inp["diff_lk1"][l],
                              inp["diff_lq2"][l], inp["diff_lk2"][l], inp["sgu_b"][l].reshape(-1)])
        out[l] = np.broadcast_to(row[None, :], (128, 1280))
    return out


def pack_consts():
    c = np.zeros((128, 512 + 128 + 1024), np.float32)
    k = np.arange(128)[:, None]
    q = np.arange(256)[None, :]
    for j in range(2):
        c[:, j * 256:(j + 1) * 256] = ((128 * j + k) <= q).astype(np.float32)
    c[:, 512:640] = np.eye(128, dtype=np.float32)
    ind = np.zeros((8, 8, 128), np.float32)
    for n in range(8):
        ind[n, n, :] = 1.0
    c[:8, 640:1664] = ind.reshape(8, 1024)
    return c


class Tile:
    __slots__ = ("ap", "w", "rc", "rd", "name")

    def __init__(self, ap, name=""):
        self.ap = ap
        self.w = None
        self.rc = {}
        self.rd = []
        self.name = name


class Sub:
    __slots__ = ("ap", "parent")

    def __init__(self, ap, parent):
        self.ap = ap
        self.parent = parent


class Op:
    __slots__ = ("eng", "fn", "deps", "idx", "need_inc", "dma", "sem", "target", "waits", "tick")

    def __init__(self, eng, fn, dma):
        self.eng = eng
        self.fn = fn
        self.dma = dma
        self.deps = []
        self.need_inc = False
        self.sem = None
        self.target = 0
        self.waits = []
        self.tick = 0


ENGS = ("pe", "act", "dve", "pool", "sp")


class Prog:
    def __init__(self, nring=24):
        self.ops = {e: [] for e in ENGS}
        self.nring = nring
        self.ring_last = [None] * nring
        self.ring_cnt = [0] * nring
        self.ndma = 0
        self.npool = 0
        self.nsp = 0
        self.tiles = []
        self.pending_dma = []

    def tile(self, ap, name=""):
        t = Tile(ap, name)
        self.tiles.append(t)
        return t

    def op(self, eng, fn, reads=(), writes=(), dma=False, extra=()):
        o = Op(eng, fn, dma)
        deps = []
        writes = [t.parent if isinstance(t, Sub) else t for t in writes] + [t.parent for t in reads if isinstance(t, Sub)]
        reads = [t for t in reads if not isinstance(t, Sub)]
        for t in reads:
            if t.w is not None:
                deps.append(t.w)
        for t in writes:
            if t.w is not None:
                deps.append(t.w)
            deps.extend(t.rc.values())
            deps.extend(t.rd)
        deps.extend(extra)
        if dma:
            half = self.nring // 2
            if eng == "pool":
                r = half + (self.npool % (self.nring - half))
                self.npool += 1
            else:
                r = self.nsp % half
                self.nsp += 1
            self.ndma += 1
            if self.ring_last[r] is not None:
                deps.append(self.ring_last[r])
            self.ring_cnt[r] += 1
            o.sem = r
            o.target = 16 * self.ring_cnt[r]
            self.ring_last[r] = o
            self.pending_dma.append(o)
        o.deps = [d for d in deps if d is not o]
        for t in reads:
            if dma:
                t.rd.append(o)
            else:
                t.rc[eng] = o
        for t in writes:
            t.w = o
            t.rc = {}
            t.rd = []
        o.idx = len(self.ops[eng])
        self.ops[eng].append(o)
        return o

    def barrier(self):
        lasts = [self.ops[e][-1] for e in ENGS if self.ops[e]]
        pend = list(self.pending_dma)
        for e in ENGS:
            self.op(e, lambda en: en.nop(), extra=lasts + pend)
        self.pending_dma = []
        for t in self.tiles:
            t.w = None
            t.rc = {}
            t.rd = []

    def finalize(self):
        for e in ENGS:
            seen = {a: -1 for a in ENGS}
            seen_d = {}
            for o in self.ops[e]:
                for d in o.deps:
                    if d.dma:
                        if seen_d.get(d.sem, 0) >= d.target:
                            continue
                        seen_d[d.sem] = d.target
                        o.waits.append(("d", d.sem, d.target))
                    else:
                        if d.eng == e and e == "pe":
                            continue
                        if seen[d.eng] >= d.idx:
                            continue
                        seen[d.eng] = d.idx
                        d.need_inc = True
                        o.waits.append(("c", d.eng, d))
        for e in ENGS:
            n = 0
            for o in self.ops[e]:
                if o.need_inc and not o.dma:
                    n += 1
                    o.tick = n

    def emit(self, nc, esems, dsems):
        handles = {"pe": "tensor", "act": "scalar", "dve": "vector", "pool": "gpsimd", "sp": "sync"}
        with nc.Block() as block:
            for e in ENGS:
                def body(en, e=e):
                    for o in self.ops[e]:
                        for w in o.waits:
                            if w[0] == "d":
                                en.wait_ge(dsems[w[1]], w[2])
                            else:
                                en.wait_ge(esems[w[1]], w[2].tick)
                        ins = o.fn(en)
                        if o.dma:
                            ins.then_inc(dsems[o.sem], 16)
                        elif o.need_inc:
                            ins.then_inc(esems[e], 1)
                getattr(block, handles[e])(body)


class Pool:
    def __init__(self, tiles):
        self.t = tiles
        self.i = 0

    def get(self):
        t = self.t[self.i % len(self.t)]
        self.i += 1
        return t


import os
DBG = set(os.environ.get("KDBG", "").split(","))


def build(stages, n_layers=DEPTH):
    nc = bass.Bass("TRN2", target_bir_lowering=False)
    xT = nc.dram_tensor("xT", [SPC, 128, 8, SEQ], F32, kind="ExternalInput").ap()
    memT = nc.dram_tensor("memT", [SPC, 128, 8, MEM], F32, kind="ExternalInput").ap()
    posb = nc.dram_tensor("posb", [SPC, 128, SEQ], I32, kind="ExternalInput").ap()
    wpack = nc.dram_tensor("wpack", [DEPTH, 128, LAY.tot], F32, kind="ExternalInput").ap()
    params = nc.dram_tensor("params", [128, NPAR], F32, kind="ExternalInput").ap()
    bcast = nc.dram_tensor("bcast", [DEPTH, 128, 1280], F32, kind="ExternalInput").ap()
    consts = nc.dram_tensor("consts", [128, 1664], F32, kind="ExternalInput").ap()
    outT = nc.dram_tensor("outT", [SPC, 128, 8, SEQ], F32, kind="ExternalOutput").ap()

    P = Prog()
    import contextlib
    es = contextlib.ExitStack()

    def sb(name, shape, dt):
        return es.enter_context(nc.sbuf_tensor(name, shape, dt))

    def ps(name, shape, dt):
        return es.enter_context(nc.psum_tensor(name, shape, dt))

    with es:
        XH = sb("XH", [128, 8, SEQ], BF16)
        XL = sb("XL", [128, 8, SEQ], BF16)
        BIGT = sb("BIGT", [128, 16384], F32)
        WP = sb("WP", [128, NSLOT, SLOT], BF16)
        PAR = sb("PAR", [128, NPAR], F32)
        BC = sb("BC", [128, 1280], F32)
        MASK = sb("MASK", [128, 2, 256], BF16)
        IDENT = sb("IDENT", [128, 128], BF16)
        IND = sb("IND", [128, 8, 128], BF16)
        ONESB = sb("ONESB", [128, 128], BF16)
        ONESF = sb("ONESF", [128, 128], F32)
        WST = sb("WST", [128, 4, 128], BF16)
        SGB = sb("SGB", [128, 512], BF16)
        LAM = sb("LAM", [128, 8], F32)
        TF = sb("TF", [128, 8, NB], F32)
        TB = sb("TB", [128, 8, NB], BF16)
        STG = sb("STG", [128, 1, 8, NB], F32)
        SCR = sb("SCR", [128, 8768], BF16)
        DED = sb("DED", [128, 4, NB], F32)
        KBAR = sb("KBAR", [128, 2, 8], BF16)
        KBF = sb("KBF", [128, 2, 8], F32)
        SMALL = sb("SMALL", [128, 64], F32)
        PSF = [ps(f"PSF{i}", [128, 2, NB], F32) for i in range(8)]
        PSH = PSF[7][:, :, :].bitcast(BF16)

        xh_t = [[P.tile(None, f"xh{b}_{m}") for m in range(8)] for b in range(NBLK)]
        wslots = Pool([P.tile(WP[:, i, :], f"w{i}") for i in range(NSLOT)])
        bank_t = [P.tile(None, f"bank{i}") for i in range(8)]
        PSA = Pool([Sub(PSF[bk][:, h, :], bank_t[bk]) for h in range(2) for bk in (0, 1, 2)])
        PSB = Pool([Sub(PSF[bk][:, h, :], bank_t[bk]) for h in range(2) for bk in (3, 4, 5, 6)])
        PSC = Pool([Sub(PSF[7][:, h, :], bank_t[7]) for h in range(2)])
        psh_t = bank_t[7]
        PSA2 = Pool([Sub(PSF[bk][:, :, :], bank_t[bk]) for bk in (0, 1, 2)])
        PSB2 = Pool([Sub(PSF[bk][:, :, :], bank_t[bk]) for bk in (3, 4, 5, 6)])
        tf = Pool([P.tile(TF[:, i, :], f"tf{i}") for i in range(8)])
        tb = Pool([P.tile(TB[:, i, :], f"tb{i}") for i in range(8)])
        tb2 = Pool([P.tile(TB[:, 2 * i:2 * i + 2, :], f"tbp{i}") for i in range(4)])
        stg = Pool([P.tile(STG[:, i], f"stg{i}") for i in range(1)])
        ded = [P.tile(DED[:, i, :], f"ded{i}") for i in range(4)]
        print("sbuf bytes remaining", nc.sbuf_bytes_remaining)
        par_t = P.tile(PAR, "par")
        bc_t = P.tile(BC, "bc")
        const_t = P.tile(None, "const")
        lam_t = P.tile(LAM, "lam")
        small_t = P.tile(SMALL, "small")
        acc_t = [[P.tile(None, f"acc{b}_{m}") for m in range(8)] for b in range(NBLK)]
        ka_t = [[P.tile(None, f"ka{b}_{c}") for c in range(4)] for b in range(NBLK)]
        kc_t = [[P.tile(None, f"kc{b}_{c}") for c in range(2)] for b in range(NBLK)]
        v_t = [[[P.tile(None, f"v{b}_{t}_{p}") for p in range(3)] for t in range(2)] for b in range(NBLK)]
        zz_t = [P.tile(None, f"zz{m}") for m in range(8)]
        zz2_t = [P.tile(None, f"zzb{m}") for m in range(8)]
        scr_t = {}

        def scr(name):
            if name not in scr_t:
                scr_t[name] = P.tile(None, "scr_" + name)
            return scr_t[name]

        BIGB = BIGT[:, :].bitcast(BF16)
        ACC = BIGT[:, :].rearrange("p (c t) -> p c t", c=8)
        KA = BIGB[:, 0:8192].rearrange("p (c t) -> p c t", c=4)
        KC = BIGB[:, 8192:12288].rearrange("p (c t) -> p c t", c=2)
        VV = BIGB[:, 12288:24576].rearrange("p (t c) -> p t c", t=16)
        ZZ = BIGT[:, 12288:14336].rearrange("p (c t) -> p c t", c=8)
        ZZ2 = BIGT[:, 14336:16384].rearrange("p (c t) -> p c t", c=8)
        QA = SCR[:, 0:2048].rearrange("p (c t) -> p c t", c=4)
        QC = SCR[:, 2048:3072].rearrange("p (c t) -> p c t", c=2)
        UU = SCR[:, 3072:4096].rearrange("p (c t) -> p c t", c=4)
        VS = SCR[:, 4096:4608].rearrange("p (t c) -> p t c", t=2)
        MIX = SCR[:, 4608:7168].rearrange("p (c t) -> p c t", c=10)
        BIAST = SCR[:, 7680:8704].rearrange("p (h t) -> p h t", h=4)
        BIASQ = SCR[:, 8704:8768].rearrange("p (t h n) -> p t h n", t=2, h=4)
        QX = SCR[:, 0:2048].rearrange("p (c t) -> p c t", c=8)
        XO = SCR[:, 2048:4096].rearrange("p (c t) -> p c t", c=8)
        MEMB = BIGB[:, 0:2048].rearrange("p (c t) -> p c t", c=8)
        KM = BIGB[:, 2048:4096].rearrange("p (c t) -> p c t", c=8)
        VM = BIGB[:, 4096:6144].rearrange("p (t c) -> p t c", t=2)

        def dma_in(eng, out_ap, in_ap, wt, rt=()):
            return P.op(eng, lambda e: e.dma_start(out=out_ap, in_=in_ap), reads=rt, writes=wt, dma=True)

        def wload(l, name):
            o, s = LAY.off[name]
            t = wslots.get()
            dma_in("pool", t.ap[:, 0:s], wpack[l, :, o:o + s], [t])
            return t

        def mm(out_t, out_ap, lt, l_ap, rt, r_ap, start, stop):
            rd = [x for x in (lt, rt) if x is not None]
            if not start:
                rd.append(out_t)
            P.op("pe", lambda e: e.matmul(out_ap, l_ap, r_ap, start=start, stop=stop), reads=rd, writes=[out_t])

        def act(out_t, out_ap, in_ts, in_ap, func, bias=0.0, scale=1.0, extra_w=()):
            P.op("act", lambda e: e.activation(out_ap, in_ap, func, bias=bias, scale=scale),
                 reads=in_ts, writes=[out_t] + list(extra_w))

        def tt(eng, out_t, out_ap, rts, a_ap, b_ap, op):
            P.op(eng, lambda e: e.tensor_tensor(out_ap, a_ap, b_ap, op), reads=rts, writes=[out_t])

        def stt(eng, out_t, out_ap, rts, a_ap, scalar, b_ap, op0, op1):
            P.op(eng, lambda e: e.scalar_tensor_tensor(out_ap, a_ap, scalar, b_ap, op0, op1), reads=rts, writes=[out_t])

        def ts(eng, out_t, out_ap, rts, a_ap, s1, s2, op0, op1=None):
            if op1 is None:
                P.op(eng, lambda e: e.tensor_scalar(out_ap, a_ap, s1, None, op0), reads=rts, writes=[out_t])
            else:
                P.op(eng, lambda e: e.tensor_scalar(out_ap, a_ap, s1, s2, op0, op1), reads=rts, writes=[out_t])

        def rsqrt(out_t, out_ap, rts, in_ap, mul, add):
            ts("dve", out_t, out_ap, rts, in_ap, mul, add, ALU.mult, ALU.add)
            P.op("act", lambda e: e.activation(out_ap, out_ap, AF.Ln), reads=[out_t], writes=[out_t])
            P.op("act", lambda e: e.activation(out_ap, out_ap, AF.Exp, scale=-0.5), reads=[out_t], writes=[out_t])

        def recip(out_t, out_ap, in_t, in_ap):
            P.op("act", lambda e: e.activation(out_ap, in_ap, AF.Ln), reads=[in_t], writes=[out_t])
            P.op("act", lambda e: e.activation(out_ap, out_ap, AF.Exp, scale=-1.0), reads=[out_t], writes=[out_t])

        def cp(eng, out_t, out_ap, rts, in_ap):
            if eng == "act":
                P.op("act", lambda e: e.copy(out_ap, in_ap), reads=rts, writes=[out_t])
            else:
                P.op(eng, lambda e: e.tensor_copy(out_ap, in_ap), reads=rts, writes=[out_t])

        def bs(b):
            return slice(b * NB, (b + 1) * NB)

        def pcol(key, m=0):
            c = PCOLS[key] + m
            return PAR[:, c:c + 1]

        dma_in("sp", PAR[:, :], params[:, :], [par_t])
        dma_in("pool", MASK[:, :, :], consts[:, 0:512].rearrange("p (j t) -> p j t", j=2), [const_t])
        dma_in("pool", IDENT[:, :], consts[:, 512:640], [const_t])
        dma_in("pool", IND[0:8, :, :], consts[0:8, 640:1664].rearrange("p (b t) -> p b t", b=8), [const_t])
        P.op("dve", lambda e: e.memset(ONESB[:, :], 1.0), writes=[const_t])
        P.op("dve", lambda e: e.memset(ONESF[:, :], 1.0), writes=[const_t])

        def split_hilo(b, m, r_t, r_ap):
            cp("act", xh_t[b][m], XH[:, m, bs(b)], [r_t], r_ap)
            tt("dve", xh_t[b][m], XL[:, m, bs(b)], [r_t, xh_t[b][m]], r_ap, XH[:, m, bs(b)], ALU.subtract)

        def load_x(s):
            for b in range(NBLK):
                st = stg.get()
                dma_in("sp", st.ap, xT[s, :, :, bs(b)], [st])
                for m in range(8):
                    cp("act", xh_t[b][m], XH[:, m, bs(b)], [st], st.ap[:, m, :])
                for m in range(8):
                    tt("dve", xh_t[b][m], XL[:, m, bs(b)], [st, xh_t[b][m]], st.ap[:, m, :], XH[:, m, bs(b)], ALU.subtract)

        def store_x(s):
            for b in range(NBLK):
                st = stg.get()
                for m in range(8):
                    tt("dve", st, st.ap[:, m, :], [xh_t[b][m]], XH[:, m, bs(b)], XL[:, m, bs(b)], ALU.add)
                P.op("sp", lambda e, st=st, b=b: e.dma_start(out=outT[s, :, :, bs(b)], in_=st.ap), reads=[st], dma=True)

        ln_sums = {}

        def ln_A(l, which, b, ysrc, zt, z_ap, dset=0):
            ln_A1(l, which, b, ysrc, zt, z_ap, dset)
            ln_A2(l, which, b, zt, z_ap, dset)

        def ln_A1(l, which, b, ysrc, zt, z_ap, dset=0):
            s1 = PSC.get()
            s2 = PSC.get()
            ln_sums[dset] = (s1, s2)
            for m in range(8):
                yt, yap = ysrc(m)
                stt("dve", zt[m], z_ap[:, m, :], [xh_t[b][m], yt], XH[:, m, bs(b)], ALPHA, yap, ALU.mult, ALU.add)
            for m in range(8):
                stt("dve", zt[m], z_ap[:, m, :], [xh_t[b][m], zt[m]], XL[:, m, bs(b)], ALPHA, z_ap[:, m, :], ALU.mult, ALU.add)
                mm(s1, s1.ap, const_t, ONESF[:, :], zt[m], z_ap[:, m, :], m == 0, m == 7)
            for m in range(8):
                q = tf.get()
                act(q, q.ap, [zt[m]], z_ap[:, m, :], AF.Square)
                mm(s2, s2.ap, const_t, ONESF[:, :], q, q.ap, m == 0, m == 7)

        def ln_A2(l, which, b, zt, z_ap, dset=0):
            s1, s2 = ln_sums[dset]
            mean = ded[2 * dset]
            ts("dve", mean, mean.ap, [s1], s1.ap, 1.0 / D, None, ALU.mult)
            msq = tf.get()
            tt("dve", msq, msq.ap, [mean], mean.ap, mean.ap, ALU.mult)
            var = tf.get()
            stt("dve", var, var.ap, [s2, msq], s2.ap, 1.0 / D, msq.ap, ALU.mult, ALU.subtract)
            rstd = ded[2 * dset + 1]
            rsqrt(rstd, rstd.ap, [var], var.ap, 1.0, EPS)
            tt("dve", mean, mean.ap, [mean, rstd], mean.ap, rstd.ap, ALU.mult)

        def ln_B(l, which, b, zt, z_ap, dset=0):
            mean = ded[2 * dset]
            rstd = ded[2 * dset + 1]
            for m in range(8):
                tt("dve", zt[m], z_ap[:, m, :], [zt[m], rstd], z_ap[:, m, :], rstd.ap, ALU.mult)
            for m in range(8):
                tt("dve", zt[m], z_ap[:, m, :], [zt[m], mean], z_ap[:, m, :], mean.ap, ALU.subtract)
            for m in range(8):
                act(xh_t[b][m], XH[:, m, bs(b)], [zt[m], par_t], z_ap[:, m, :], AF.Identity,
                    bias=pcol((l, which + "_b"), m), scale=pcol((l, which + "_g"), m))
            for m in range(8):
                act(zt[m], z_ap[:, m, :], [zt[m], par_t], z_ap[:, m, :], AF.Identity,
                    bias=pcol((l, which + "_b"), m), scale=pcol((l, which + "_g"), m))
            for m in range(8):
                tt("dve", xh_t[b][m], XL[:, m, bs(b)], [zt[m], xh_t[b][m]], z_ap[:, m, :], XH[:, m, bs(b)], ALU.subtract)

        def ln_block(l, which, b, ysrc, zt, z_ap):
            ln_A(l, which, b, ysrc, zt, z_ap, 0)
            ln_B(l, which, b, zt, z_ap, 0)

        def ffn(l, f, which):
            rem = NJ % JG
            groups = ([list(range(0, rem))] if rem else []) + [list(range(j, j + JG)) for j in range(rem, NJ, JG)]

            def lnA1(b):
                ln_A1(l, which, b, lambda m, b=b: (acc_t[b][m], ACC[:, m, bs(b)]), acc_t[b], ACC[:, :, bs(b)], b % 2)

            def lnA2(b):
                ln_A2(l, which, b, acc_t[b], ACC[:, :, bs(b)], b % 2)

            def lnB(b):
                ln_B(l, which, b, acc_t[b], ACC[:, :, bs(b)], b % 2)
            if "g1" in DBG:
                groups = groups[:1]
            for gi, grp in enumerate(groups):
                wg = {j: wload(l, f"g{f}_{j}") for j in grp}
                wu = {j: wload(l, f"u{f}_{j}") for j in grp}
                wd = {j: wload(l, f"d{f}_{j}") for j in grp}
                for b in range(NBLK if "nomm" not in DBG else 0):
                    hs = {}
                    for j in grp:
                        pg = PSA.get()
                        pu = PSA.get()
                        for k in range(8):
                            mm(pg, pg.ap, wg[j], wg[j].ap[:, k * 128:(k + 1) * 128], xh_t[b][k], XH[:, k, bs(b)], k == 0, k == 7)
                        for k in range(8):
                            mm(pu, pu.ap, wu[j], wu[j].ap[:, k * 128:(k + 1) * 128], xh_t[b][k], XH[:, k, bs(b)], k == 0, k == 7)
                        if "nocons" in DBG:
                            continue
                        sg = tf.get()
                        if "nosilu" in DBG:
                            cp("act", sg, sg.ap, [pg], pg.ap)
                        else:
                            act(sg, sg.ap, [pg], pg.ap, AF.Silu)
                        h = tb.get()
                        if "nott" in DBG:
                            cp("dve", h, h.ap, [sg], sg.ap)
                            cp("dve", h, h.ap, [pu], pu.ap)
                        else:
                            tt("dve", h, h.ap, [sg, pu], sg.ap, pu.ap, ALU.mult)
                        hs[j] = h
                    pd = [PSB.get() for _ in range(8)]
                    for m in range(8 if "nodown" not in DBG else 0):
                        for ji, j in enumerate(grp):
                            mm(pd[m], pd[m].ap, wd[j], wd[j].ap[:, m * 128:(m + 1) * 128], hs[j], hs[j].ap, ji == 0, ji == len(grp) - 1)
                    for m in range(8):
                        if gi == 0:
                            act(acc_t[b][m], ACC[:, m, bs(b)], [pd[m]], pd[m].ap, AF.Copy, scale=0.5)
                        else:
                            stt("dve", acc_t[b][m], ACC[:, m, bs(b)], [pd[m], acc_t[b][m]], pd[m].ap, 0.5, ACC[:, m, bs(b)], ALU.mult, ALU.add)
                    if gi == len(groups) - 1:
                        if b >= 1:
                            lnA1(b - 1)
                        if b >= 2:
                            lnB(b - 2)
                        if b >= 1:
                            lnA2(b - 1)
            lnA1(NBLK - 1)
            lnB(NBLK - 2)
            lnA2(NBLK - 1)
            lnB(NBLK - 1)

        def rope_tables(s, b, dset=1):
            pi_t = tf.get()
            P.op("sp", lambda e: e.dma_start(out=pi_t.ap.bitcast(I32), in_=posb[s, :, bs(b)]), writes=[pi_t], dma=True)
            ang = tf.get()
            cp("dve", ang, ang.ap, [pi_t], pi_t.ap.bitcast(I32))
            ts("dve", ang, ang.ap, [ang, par_t], ang.ap, pcol("invf"), None, ALU.mult)
            C1 = 6.28125
            C2 = 2.0 * math.pi - C1

            def reduce(add):
                t = tf.get()
                ki = tf.get()
                if add:
                    ts("dve", t, t.ap, [ang], ang.ap, 1.0 / (2.0 * math.pi), add / (2.0 * math.pi), ALU.mult, ALU.add)
                else:
                    ts("dve", t, t.ap, [ang], ang.ap, 1.0 / (2.0 * math.pi), None, ALU.mult)
                cp("dve", ki, ki.ap.bitcast(I32), [t], t.ap)
                cp("dve", t, t.ap, [ki], ki.ap.bitcast(I32))
                if add:
                    stt("dve", ki, ki.ap, [t, ang], t.ap, -C1, ang.ap, ALU.mult, ALU.add)
                    ts("dve", ki, ki.ap, [ki], ki.ap, add, None, ALU.add)
                else:
                    stt("dve", ki, ki.ap, [t, ang], t.ap, -C1, ang.ap, ALU.mult, ALU.add)
                stt("dve", ki, ki.ap, [t, ki], t.ap, -C2, ki.ap, ALU.mult, ALU.add)
                ts("dve", ki, ki.ap, [ki], ki.ap, -math.pi, None, ALU.max)
                ts("dve", ki, ki.ap, [ki], ki.ap, math.pi, None, ALU.min)
                return ki
            sn = ded[2 * dset + 1]
            cs = ded[2 * dset]
            r1 = reduce(0.0)
            act(sn, sn.ap, [r1], r1.ap, AF.Sin)
            r2 = reduce(0.5 * math.pi)
            act(cs, cs.ap, [r2], r2.ap, AF.Sin)
            ts("dve", sn, sn.ap, [sn, par_t], sn.ap, pcol("sign"), None, ALU.mult)
            ts("dve", sn, sn.ap, [sn], sn.ap, -1.0, None, ALU.mult)
            return cs, sn

        def proj_rope(l, wa, was, b, cs, sn, out_t, out_ap, bd=False):
            p1 = PSA.get()
            p2 = PSA.get()
            for k in range(8):
                mm(p1, p1.ap, wa, wa.ap[:, k * 128:(k + 1) * 128], xh_t[b][k], XH[:, k, bs(b)], k == 0, k == 7)
            for k in range(8):
                mm(p2, p2.ap, was, was.ap[:, k * 128:(k + 1) * 128], xh_t[b][k], XH[:, k, bs(b)], k == 0, k == 7)
            t1 = tf.get()
            tt("dve", t1, t1.ap, [p1, cs], p1.ap, cs.ap, ALU.mult)
            t2 = tf.get()
            tt("dve", t2, t2.ap, [p2, sn], p2.ap, sn.ap, ALU.mult)
            if bd:
                tt("dve", out_t, out_ap[0:64, 0:NB], [t1, t2], t1.ap[0:64, :], t2.ap[0:64, :], ALU.add)
                tt("dve", out_t, out_ap[64:128, NB:2 * NB], [t1, t2], t1.ap[64:128, :], t2.ap[64:128, :], ALU.add)
            else:
                tt("dve", out_t, out_ap, [t1, t2], t1.ap, t2.ap, ALU.add)

        def mixer_consts(l):
            P.op("dve", lambda e: e.memset(SCR[:, 0:3072], 0.0), writes=[scr("qa"), scr("qc")])
            dma_in("sp", BC[:, :], bcast[l, :, :], [bc_t])
            o, s_ = LAY.off["wst"]
            wt = wslots.get()
            dma_in("pool", wt.ap[:, 0:512], wpack[l, :, o:o + 512], [wt])
            for g in range(4):
                tt("dve", const_t, WST[:, g, :], [wt, const_t], wt.ap[:, g * 128:(g + 1) * 128], MASK[:, 0, 0:128], ALU.mult)
            cp("dve", const_t, SGB[:, :], [bc_t], BC[:, 768:1280])
            lam_init = 0.8 - 0.6 * math.exp(-0.3 * l)
            t = tf.get()
            tt("dve", t, t.ap[:, 0:64], [bc_t], BC[:, 512:576], BC[:, 576:640], ALU.mult)
            tt("dve", t, t.ap[:, 64:128], [bc_t], BC[:, 640:704], BC[:, 704:768], ALU.mult)
            P.op("dve", lambda e: e.reduce_sum(LAM[:, 0:1], t.ap[:, 0:64], AX.X), reads=[t], writes=[lam_t])
            P.op("dve", lambda e: e.reduce_sum(LAM[:, 1:2], t.ap[:, 64:128], AX.X), reads=[t], writes=[lam_t])
            act(lam_t, LAM[:, 2:4], [lam_t], LAM[:, 0:2], AF.Exp)
            tt("dve", lam_t, LAM[:, 4:5], [lam_t], LAM[:, 3:4], LAM[:, 2:3], ALU.subtract)
            ts("dve", lam_t, LAM[:, 4:5], [lam_t], LAM[:, 4:5], -lam_init, None, ALU.add)
            ts("dve", lam_t, LAM[:, 5:6], [par_t], pcol((l, "subln")), 1.0 - lam_init, None, ALU.mult)

        def mixer_phase1(l, s):
            wv = [wload(l, f"wv_{k}") for k in range(8)]
            for b in range(NBLK):
                for tI in range(2):
                    tok = slice(b * NB + tI * 128, b * NB + (tI + 1) * 128)
                    pa = PSA.get()
                    pb = PSA.get()
                    for k in range(8):
                        mm(pa, pa.ap, xh_t[b][k], XH[:, k, tok], wv[k], wv[k].ap[:, 0:256], k == 0, k == 7)
                    for k in range(8):
                        mm(pb, pb.ap, xh_t[b][k], XH[:, k, tok], wv[k], wv[k].ap[:, 256:512], k == 0, k == 7)
                    cp("act", v_t[b][tI][0], VV[:, 2 * b + tI, 0:256], [pa], pa.ap)
                    cp("act", v_t[b][tI][1], VV[:, 2 * b + tI, 256:512], [pb], pb.ap)
                    pc = PSA.get()
                    for k in range(8):
                        mm(pc, pc.ap, xh_t[b][k], XH[:, k, tok], wv[k], wv[k].ap[:, 512:768], k == 0, k == 7)
                    cp("act", v_t[b][tI][2], VV[:, 2 * b + tI, 512:768], [pc], pc.ap)
            kw = []
            for c in range(6):
                if c < 4:
                    kw.append((wload(l, f"ka_{c}"), wload(l, f"kas_{c}")))
                else:
                    kw.append((wload(l, f"kc_{c - 4}"), wload(l, f"kcs_{c - 4}")))
            tabs = {0: rope_tables(s, 0, 0)}
            for b in range(NBLK):
                if b + 1 < NBLK:
                    tabs[b + 1] = rope_tables(s, b + 1, (b + 1) % 2)
                cs, sn = tabs[b]
                for c in range(6):
                    dst = KA[:, c, bs(b)] if c < 4 else KC[:, c - 4, bs(b)]
                    proj_rope(l, kw[c][0], kw[c][1], b, cs, sn, ka_t[b][c] if c < 4 else kc_t[b][c - 4], dst)
            for c in range(2):
                for b in range(NBLK):
                    P.op("dve", lambda e, c=c, b=b: e.reduce_sum(KBF[:, c, b:b + 1], KC[:, c, bs(b)], AX.X),
                         reads=[kc_t[b][c]], writes=[small_t])
            ts("dve", small_t, KBAR[:, :, :], [small_t], KBF[:, :, :], 1.0 / 256.0, None, ALU.mult)

        def run_streams(streams, nkt):
            pend = [sc(0) for sc, _ in streams]
            for kt in range(nkt):
                for i, (sc, fin) in enumerate(streams):
                    nx = sc(kt + 1) if kt + 1 < nkt else None
                    fin(kt, pend[i])
                    pend[i] = nx

        def mixer_phase2(l, s):
            lam_init = 0.8 - 0.6 * math.exp(-0.3 * l)
            def front(b):
                cs, sn = rope_tables(s, b)
                for c in range(4):
                    wa, was = wload(l, f"qa_{c}"), wload(l, f"qas_{c}")
                    proj_rope(l, wa, was, b, cs, sn, scr("qa"), QA[:, c, :], bd=True)
                for c in range(2):
                    wa, was = wload(l, f"qc_{c}"), wload(l, f"qcs_{c}")
                    proj_rope(l, wa, was, b, cs, sn, scr("qc"), QC[:, c, :], bd=True)
                for g in range(4 if "nosgu" not in DBG else 0):
                    w = wload(l, f"u_{g}")
                    p = PSA.get()
                    for k in range(8):
                        mm(p, p.ap[0:64, :], w, w.ap[:, k * 64:(k + 1) * 64], xh_t[b][k], XH[:, k, bs(b)], k == 0, k == 7)
                    act(scr("u"), UU[0:64, g, :], [p], p.ap[0:64, :], AF.Gelu)
                wvs = [wload(l, f"vs_{h}") for h in range(2)]
                for tI in range(2 if "nosgu" not in DBG else 0):
                    tok = slice(b * NB + tI * 128, b * NB + (tI + 1) * 128)
                    p = PSA.get()
                    for k in range(8):
                        w = wvs[k // 4]
                        mm(p, p.ap, xh_t[b][k], XH[:, k, tok], w, w.ap[:, (k % 4) * 256:(k % 4 + 1) * 256], k == 0, k == 7)
                    gv = tf.get()
                    act(gv, gv.ap, [p], p.ap, AF.Gelu)
                    sq = tf.get()
                    P.op("dve", lambda e, gv=gv: e.reduce_sum(SMALL[:, 0:1], gv.ap, AX.X), reads=[gv], writes=[small_t])
                    act(sq, sq.ap, [gv], gv.ap, AF.Square)
                    P.op("dve", lambda e, sq=sq: e.reduce_sum(SMALL[:, 1:2], sq.ap, AX.X), reads=[sq], writes=[small_t])
                    ts("dve", small_t, SMALL[:, 2:3], [small_t], SMALL[:, 0:1], 1.0 / 256.0, None, ALU.mult)
                    tt("dve", small_t, SMALL[:, 3:4], [small_t], SMALL[:, 2:3], SMALL[:, 2:3], ALU.mult)
                    stt("dve", small_t, SMALL[:, 4:5], [small_t], SMALL[:, 1:2], 1.0 / 256.0, SMALL[:, 3:4], ALU.mult, ALU.subtract)
                    rsqrt(small_t, SMALL[:, 5:6], [small_t], SMALL[:, 4:5], 1.0, EPS)
                    ts("dve", gv, gv.ap, [gv, small_t], gv.ap, SMALL[:, 2:3], None, ALU.subtract)
                    ts("dve", gv, gv.ap, [gv, small_t], gv.ap, SMALL[:, 5:6], None, ALU.mult)
                    tt("dve", gv, gv.ap, [gv, bc_t], gv.ap, BC[:, 0:256], ALU.mult)
                    tt("dve", scr("vs"), VS[:, tI, :], [gv, bc_t], gv.ap, BC[:, 256:512], ALU.add)
                for tI in range(2 if "nogate" not in DBG else 0):
                    qblk = b
                    pg = PSA.get()
                    for h in range(4):
                        rows = slice((h % 2) * 64, (h % 2) * 64 + 64)
                        mm(pg, pg.ap[:, h * 8:(h + 1) * 8], scr("qc"), QC[rows, h // 2, (h % 2) * NB + tI * 128:(h % 2) * NB + (tI + 1) * 128],
                           small_t, KBAR[rows, h // 2, :], True, True)
                    gt = tf.get()
                    cp("dve", gt, gt.ap[:, 0:32], [pg], pg.ap[:, 0:32])
                    g3 = gt.ap[:, 0:32].rearrange("p (h n) -> p h n", h=4)
                    if qblk < 8:
                        P.op("dve", lambda e, g3=g3, qblk=qblk: e.memset(g3[:, :, qblk:8], -1e30), reads=[gt], writes=[gt])
                    for h in range(4):
                        P.op("dve", lambda e, gt=gt, h=h: e.max(gt.ap[:, 64 + h * 8:72 + h * 8], gt.ap[:, h * 8:(h + 1) * 8]),
                             reads=[gt], writes=[gt])
                    for h in range(4):
                        ts("dve", gt, gt.ap[:, 128 + h * 8:136 + h * 8], [gt], gt.ap[:, h * 8:(h + 1) * 8],
                           gt.ap[:, 64 + h * 8 + 2:64 + h * 8 + 3], None, ALU.is_ge)
                    ts("dve", gt, gt.ap[:, 160:192], [gt], gt.ap[:, 128:160], BIG, -BIG, ALU.mult, ALU.add)
                    b3 = gt.ap[:, 160:192].rearrange("p (h n) -> p h n", h=4)
                    P.op("dve", lambda e, b3=b3, qblk=qblk: e.memset(b3[:, :, qblk:8], 0.0), reads=[gt], writes=[gt])
                    cp("dve", scr("biasq"), BIASQ[:, tI, :, :], [gt], b3)
                for h in range(4 if "nogate" not in DBG else 0):
                    p = PSA.get()
                    for tI in range(2):
                        mm(p, p.ap[0:8, tI * 128:(tI + 1) * 128], scr("biasq"), BIASQ[:, tI, h, :], const_t, IDENT[:, :], True, True)
                    cp("dve", scr("biast"), BIAST[0:8, h, :], [p], p.ap[0:8, :])

            def mid(b):
                nkt = 2 * b + 2
                for g in range(4 if "nosgu" not in DBG else 0):
                    p = PSA.get()
                    for tI in range(2):
                        cols = slice(tI * 128, (tI + 1) * 128)
                        mm(p, p.ap[0:64, cols], scr("vs"), VS[:, tI, g * 64:(g + 1) * 64], const_t, WST[:, g, :], True, True)
                    sm = tf.get()
                    for tI in range(2):
                        cols = slice(tI * 128, (tI + 1) * 128)
                        tt("dve", sm, sm.ap[0:64, cols], [p, bc_t], p.ap[0:64, cols], BC[0:64, 768 + g * 128:768 + (g + 1) * 128], ALU.add)
                    tt("dve", scr("mix"), MIX[0:64, 4 + g, :], [scr("u"), sm], UU[0:64, g, :], sm.ap[0:64, :], ALU.mult)
                for h0 in range(0, 4 if "noA" not in DBG else 0, 2):
                    streams = []
                    accs = []
                    for h in (h0, h0 + 1):
                        obank = PSB2.get()
                        sbank = PSB2.get()
                        accs.append((h, obank, sbank))

                        def a_score(kt, h=h):
                            bank = PSA2.get()
                            mm(bank, bank.ap.rearrange("p a t -> p (a t)"), ka_t[kt // 2][h], KA[:, h, kt * 128:(kt + 1) * 128],
                               scr("qa"), QA[:, h, :], True, True)
                            return bank

                        def a_finish(kt, bank, h=h, obank=obank, sbank=sbank, b=b, nkt=nkt):
                            vt = v_t[kt // 2][kt % 2][h // 2]
                            pt = tb2.get()
                            act(pt, pt.ap, [bank], bank.ap, AF.Exp, scale=0.125)
                            if kt >= 2 * b:
                                for mp in range(2):
                                    tt("dve", pt, pt.ap[:, mp, :], [pt, const_t], pt.ap[:, mp, :], MASK[:, kt - 2 * b, :], ALU.mult)
                            p2d = pt.ap.rearrange("p a t -> p (a t)")
                            mm(obank, obank.ap.rearrange("p a t -> p (a t)"), vt, VV[:, kt, h * 128:(h + 1) * 128], pt, p2d, kt == 0, kt == nkt - 1)
                            mm(sbank, sbank.ap.rearrange("p a t -> p (a t)"), const_t, ONESB[:, :], pt, p2d, kt == 0, kt == nkt - 1)
                        streams.append((a_score, a_finish))
                    run_streams(streams, nkt)
                    for (h, obank, sbank) in accs:
                        po = [Sub(obank.ap[:, 0, :], obank.parent), Sub(sbank.ap[:, 0, :], sbank.parent),
                              Sub(obank.ap[:, 1, :], obank.parent), Sub(sbank.ap[:, 1, :], sbank.parent)]
                        r1 = tf.get()
                        recip(r1, r1.ap, po[1], po[1].ap)
                        r2 = tf.get()
                        recip(r2, r2.ap, po[3], po[3].ap)
                        tt("dve", r1, r1.ap, [r1, po[0]], r1.ap, po[0].ap, ALU.mult)
                        tt("dve", r2, r2.ap, [r2, po[2]], r2.ap, po[2].ap, ALU.mult)
                        o = tf.get()
                        stt("dve", o, o.ap, [r2, lam_t, r1], r2.ap, LAM[:, 4:5], r1.ap, ALU.mult, ALU.add)
                        sq = tf.get()
                        act(sq, sq.ap, [o], o.ap, AF.Square)
                        ss = PSC.get()
                        mm(ss, ss.ap, const_t, ONESF[:, :], sq, sq.ap, True, True)
                        rr = tf.get()
                        rsqrt(rr, rr.ap, [ss], ss.ap, 1.0 / 128.0, EPS)
                        tt("dve", o, o.ap, [o, rr], o.ap, rr.ap, ALU.mult)
                        ts("dve", scr("mix"), MIX[:, h, :], [o, lam_t], o.ap, LAM[:, 5:6], None, ALU.mult)
                streams = []
                accs = []
                for hp in range(2 if "noC" not in DBG else 0):
                    obank = PSB2.get()
                    sbank = PSB2.get()
                    accs.append((hp, obank, sbank))

                    def c_score(kt, hp=hp):
                        bank = PSA2.get()
                        o2d = bank.ap.rearrange("p a t -> p (a t)")
                        mm(bank, o2d, kc_t[kt // 2][hp], KC[:, hp, kt * 128:(kt + 1) * 128], scr("qc"), QC[:, hp, :], True, False)
                        mm(bank, o2d, const_t, IND[0:8, kt // 2, :], scr("biast"),
                           BIAST[0:8, 2 * hp:2 * hp + 2, :].rearrange("p h t -> p (h t)"), False, True)
                        return bank

                    def c_finish(kt, bank, hp=hp, obank=obank, sbank=sbank, b=b, nkt=nkt):
                        pt = tb2.get()
                        act(pt, pt.ap, [bank], bank.ap, AF.Exp, scale=0.125)
                        if kt >= 2 * b:
                            for hh in range(2):
                                tt("dve", pt, pt.ap[:, hh, :], [pt, const_t], pt.ap[:, hh, :], MASK[:, kt - 2 * b, :], ALU.mult)
                        vt = v_t[kt // 2][kt % 2][2]
                        p2d = pt.ap.rearrange("p a t -> p (a t)")
                        mm(obank, obank.ap.rearrange("p a t -> p (a t)"), vt, VV[:, kt, 512 + hp * 128:512 + (hp + 1) * 128], pt, p2d, kt == 0, kt == nkt - 1)
                        mm(sbank, sbank.ap.rearrange("p a t -> p (a t)"), const_t, ONESB[:, :], pt, p2d, kt == 0, kt == nkt - 1)
                    streams.append((c_score, c_finish))
                if streams:
                    run_streams(streams, nkt)
                for (hp, obank, sbank) in accs:
                    for hh in range(2):
                        rws = slice(hh * 64, hh * 64 + 64)
                        r1 = tf.get()
                        recip(r1, r1.ap[rws, :], sbank, sbank.ap[rws, hh, :])
                        tt("dve", scr("mix"), MIX[rws, 8 + hp, :], [r1, obank], r1.ap[rws, :], obank.ap[rws, hh, :], ALU.mult)

            def tail(b):
                wo = [wload(l, f"wo_{c}") for c in range(10)]
                ys = {}

                def ysrc(m, b=b, wo=wo):
                    p = PSA.get()
                    for c in range(10):
                        kk = 64 if 4 <= c < 8 else 128
                        mm(p, p.ap, wo[c], wo[c].ap[0:kk, m * 128:(m + 1) * 128], scr("mix"), MIX[0:kk, c, :], c == 0, c == 9)
                    return p, p.ap
                zt_, za_ = (zz_t, ZZ) if b % 2 == 0 else (zz2_t, ZZ2)
                ln_A1(l, "ln2", b, ysrc, zt_, za_, 0)
                if b >= 1:
                    pz, pa = (zz_t, ZZ) if (b - 1) % 2 == 0 else (zz2_t, ZZ2)
                    ln_B(l, "ln2", b - 1, pz, pa, 0)
                ln_A2(l, "ln2", b, zt_, za_, 0)


            front(0)
            mid(0)
            for b in range(NBLK):
                if b + 1 < NBLK:
                    front(b + 1)
                tail(b)
                if b + 1 < NBLK:
                    mid(b + 1)
            ln_B(l, "ln2", NBLK - 1, zz2_t, ZZ2, 0)

        def xattn(l, s):
            st = stg.get()
            dma_in("sp", st.ap[:, :, 0:MEM], memT[s, :, :, :], [st])
            cp("dve", scr("memb"), MEMB[:, :, :], [st], st.ap[:, :, 0:MEM])
            for m in range(8):
                w = wload(l, f"xk_{m}")
                p = PSA.get()
                for k in range(8):
                    mm(p, p.ap, w, w.ap[:, k * 128:(k + 1) * 128], scr("memb"), MEMB[:, k, :], k == 0, k == 7)
                cp("act", scr("km"), KM[:, m, :], [p], p.ap)
            wv = [wload(l, f"xv_{k}") for k in range(8)]
            for mt in range(2):
                for n in range(4):
                    p = PSA.get()
                    for k in range(8):
                        mm(p, p.ap, scr("memb"), MEMB[:, k, mt * 128:(mt + 1) * 128], wv[k], wv[k].ap[:, n * 256:(n + 1) * 256], k == 0, k == 7)
                    cp("act", scr("vm"), VM[:, mt, n * 256:(n + 1) * 256], [p], p.ap)
            for b in range(NBLK):
                for m in range(8):
                    w = wload(l, f"xq_{m}")
                    p = PSA.get()
                    for k in range(8):
                        mm(p, p.ap, w, w.ap[:, k * 128:(k + 1) * 128], xh_t[b][k], XH[:, k, bs(b)], k == 0, k == 7)
                    cp("act", scr("qx"), QX[:, m, :], [p], p.ap)
                def x_score(h):
                    bank = PSA2.get()
                    for mt in range(2):
                        for c in range(2):
                            mm(bank, bank.ap[:, mt, :], scr("km"), KM[:, 2 * h + c, mt * 128:(mt + 1) * 128], scr("qx"), QX[:, 2 * h + c, :], c == 0, c == 1)
                    return bank

                def x_finish(h, bank):
                    pt = tb2.get()
                    act(pt, pt.ap, [bank], bank.ap, AF.Exp, scale=1.0 / 16.0)
                    pS = PSB.get()
                    for mt in range(2):
                        mm(pS, pS.ap, const_t, ONESB[:, :], pt, pt.ap[:, mt, :], mt == 0, mt == 1)
                    r1 = tf.get()
                    recip(r1, r1.ap, pS, pS.ap)
                    for c in range(2):
                        po = PSB.get()
                        for mt in range(2):
                            mm(po, po.ap, scr("vm"), VM[:, mt, (2 * h + c) * 128:(2 * h + c + 1) * 128], pt, pt.ap[:, mt, :], mt == 0, mt == 1)
                        tt("dve", scr("xo"), XO[:, 2 * h + c, :], [r1, po], r1.ap, po.ap, ALU.mult)
                pend = x_score(0)
                for h in range(4):
                    nxt = x_score(h + 1) if h < 3 else None
                    x_finish(h, pend)
                    pend = nxt
                wo = [wload(l, f"xo_{c}") for c in range(8)]

                def ysrc(m, wo=wo):
                    p = PSA.get()
                    for c in range(8):
                        mm(p, p.ap, wo[c], wo[c].ap[:, m * 128:(m + 1) * 128], scr("xo"), XO[:, c, :], c == 0, c == 7)
                    return p, p.ap
                zt_, za_ = (zz_t, ZZ) if b % 2 == 0 else (zz2_t, ZZ2)
                ln_A1(l, "ln3", b, ysrc, zt_, za_, b % 2)
                if b >= 1:
                    pz, pa = (zz_t, ZZ) if (b - 1) % 2 == 0 else (zz2_t, ZZ2)
                    ln_B(l, "ln3", b - 1, pz, pa, (b - 1) % 2)
                ln_A2(l, "ln3", b, zt_, za_, b % 2)
            ln_B(l, "ln3", NBLK - 1, zz2_t, ZZ2, (NBLK - 1) % 2)

        for s in range(SPC):
            load_x(s)
            for (l, st_) in stages:
                P.barrier()
                if st_ == "ffn1":
                    ffn(l, 1, "ln1")
                elif st_ == "ffn2":
                    ffn(l, 2, "ln4")
                elif st_ == "mixer":
                    mixer_consts(l)
                    mixer_phase1(l, s)
                    if "p1only" not in DBG:
                        mixer_phase2(l, s)
                elif st_ == "xattn":
                    xattn(l, s)
            P.barrier()
            store_x(s)
        P.op("sp", lambda e: e.nop(), extra=list(P.pending_dma))

        P.finalize()
        import contextlib as _c
        ses = _c.ExitStack()
        with ses:
            esems = {e: ses.enter_context(nc.semaphore(f"es_{e}")) for e in ENGS}
            dsems = [ses.enter_context(nc.semaphore(f"ds_{i}")) for i in range(P.nring)]
            P.emit(nc, esems, dsems)
    return nc


FULL_STAGES = [(l, s) for l in range(DEPTH) for s in ("ffn1", "mixer", "xattn", "ffn2")]
_CACHE = {}


def kernel(stages=None, **inp):
    if stages is None:
        stages = FULL_STAGES
    inp = {k: np.asarray(v) for k, v in inp.items()}
    x = inp["x"].astype(np.float32, copy=False)
    mem = inp["mem"].astype(np.float32, copy=False)
    pos = inp["positions"].astype(np.int32, copy=False)
    xT = np.ascontiguousarray(x.reshape(BATCH, SEQ, 8, 128).transpose(0, 3, 2, 1))
    memT = np.ascontiguousarray(mem.reshape(BATCH, MEM, 8, 128).transpose(0, 3, 2, 1))
    posb = np.ascontiguousarray(np.broadcast_to(pos[:, None, :], (BATCH, 128, SEQ)))
    wpack = np.stack([pack_layer(inp, l) for l in range(DEPTH)], axis=0)
    params = pack_params(inp)
    bc = pack_bcast(inp)
    consts = pack_consts()
    key = tuple(stages)
    if key not in _CACHE:
        _CACHE[key] = build(list(stages))
    nc = _CACHE[key]
    in_maps = []
    for c in range(NCORES):
        sl = slice(c * SPC, (c + 1) * SPC)
        in_maps.append({"xT": xT[sl], "memT": memT[sl], "posb": posb[sl], "wpack": wpack,
                        "params": params, "bcast": bc, "consts": consts})
    res = run_bass_kernel_spmd(nc, in_maps, core_ids=list(range(NCORES)))
    outs = [np.asarray(r["outT"]) for r in res.results]
    oT = np.concatenate(outs, axis=0)
    out = oT.transpose(0, 3, 2, 1).reshape(BATCH, SEQ, D)
    return np.ascontiguousarray(out.astype(np.float32, copy=False))
```

```python
import math
import numpy as np
import concourse.bass as bass
import concourse.mybir as mybir
from concourse.bass_utils import run_bass_kernel_spmd

F32 = mybir.dt.float32
BF16 = mybir.dt.bfloat16
I32 = mybir.dt.int32
AF = mybir.ActivationFunctionType
ALU = mybir.AluOpType
AX = mybir.AxisListType

D = 1024
SEQ = 2048
BATCH = 16
DEPTH = 2
NCORES = 8
SPC = BATCH // NCORES
MEM = 256
DFF = 2816
NJ = DFF // 128
NB = 256
NBLK = SEQ // NB
ALPHA = (2.0 * DEPTH) ** 0.25
EPS = 1e-5
BIG = 30000.0
JG = 4
SLOT = 1024
NSLOT = 13


def _colpanel(W, c0, width):
    K = W.shape[0] // 128
    return W[:, c0:c0 + width].reshape(K, 128, width).transpose(1, 0, 2).reshape(128, K * width)


def _swap_cols(base):
    return list(range(base + 32, base + 64)) + list(range(base, base + 32))


class Layout:
    def __init__(self):
        self.off = {}
        self.tot = 0

    def add(self, name, size):
        self.off[name] = (self.tot, size)
        self.tot += size


def make_layout():
    L = Layout()
    for f in (1, 2):
        for j in range(NJ):
            L.add(f"g{f}_{j}", 1024)
            L.add(f"u{f}_{j}", 1024)
            L.add(f"d{f}_{j}", 1024)
    for c in range(4):
        L.add(f"ka_{c}", 1024)
        L.add(f"kas_{c}", 1024)
    for c in range(2):
        L.add(f"kc_{c}", 1024)
        L.add(f"kcs_{c}", 1024)
    for k in range(8):
        L.add(f"wv_{k}", 768)
    for c in range(4):
        L.add(f"qa_{c}", 1024)
        L.add(f"qas_{c}", 1024)
    for c in range(2):
        L.add(f"qc_{c}", 1024)
        L.add(f"qcs_{c}", 1024)
    for g in range(4):
        L.add(f"u_{g}", 512)
    for h in range(2):
        L.add(f"vs_{h}", 1024)
    for c in range(10):
        L.add(f"wo_{c}", 1024)
    for m in range(8):
        L.add(f"xq_{m}", 1024)
        L.add(f"xk_{m}", 1024)
    for k in range(8):
        L.add(f"xv_{k}", 1024)
    for c in range(8):
        L.add(f"xo_{c}", 1024)
    L.add("wst", 512)
    return L


LAY = make_layout()


def pack_layer(inp, l):
    out = np.zeros((128, LAY.tot), np.float32)

    def put(name, arr):
        o, s = LAY.off[name]
        assert arr.shape == (128, s), (name, arr.shape, s)
        out[:, o:o + s] = arr

    for f in (1, 2):
        wg, wu, wd = inp[f"ffn{f}_w_gate"][l], inp[f"ffn{f}_w_up"][l], inp[f"ffn{f}_w_down"][l]
        for j in range(NJ):
            put(f"g{f}_{j}", _colpanel(wg, j * 128, 128))
            put(f"u{f}_{j}", _colpanel(wu, j * 128, 128))
            put(f"d{f}_{j}", wd[j * 128:(j + 1) * 128, :])
    w_in = inp["mix_w_in"][l]
    swp = np.arange(w_in.shape[1])
    for base in list(range(0, 1024, 64)) + list(range(2048, 2560, 64)):
        swp[base:base + 64] = _swap_cols(base)
    w_sw = w_in[:, swp]
    for c in range(4):
        put(f"qa_{c}", _colpanel(w_in, c * 128, 128))
        put(f"qas_{c}", _colpanel(w_sw, c * 128, 128))
        put(f"ka_{c}", _colpanel(w_in, 512 + c * 128, 128))
        put(f"kas_{c}", _colpanel(w_sw, 512 + c * 128, 128))
    for c in range(2):
        put(f"qc_{c}", _colpanel(w_in, 2048 + c * 128, 128))
        put(f"qcs_{c}", _colpanel(w_sw, 2048 + c * 128, 128))
        put(f"kc_{c}", _colpanel(w_in, 2304 + c * 128, 128))
        put(f"kcs_{c}", _colpanel(w_sw, 2304 + c * 128, 128))
    wv = np.concatenate([w_in[:, 1024:1536], w_in[:, 2560:2816]], axis=1)
    for k in range(8):
        put(f"wv_{k}", wv[k * 128:(k + 1) * 128, :])
    for g in range(4):
        put(f"u_{g}", _colpanel(w_in, 1536 + g * 64, 64))
    wvs = w_in[:, 1792:2048]
    for h in range(2):
        put(f"vs_{h}", wvs[h * 512:(h + 1) * 512].reshape(4, 128, 256).transpose(1, 0, 2).reshape(128, 1024))
    w_out = inp["mix_w_out"][l]
    for c in range(4):
        put(f"wo_{c}", w_out[c * 128:(c + 1) * 128, :])
    for g in range(4):
        t = np.zeros((128, 1024), np.float32)
        t[:64] = w_out[512 + g * 64:512 + (g + 1) * 64, :]
        put(f"wo_{4 + g}", t)
    for hp in range(2):
        put(f"wo_{8 + hp}", w_out[768 + hp * 128:768 + (hp + 1) * 128, :])
    for m in range(8):
        put(f"xq_{m}", _colpanel(inp["xa_wq"][l], m * 128, 128))
        put(f"xk_{m}", _colpanel(inp["xa_wk"][l], m * 128, 128))
    for k in range(8):
        put(f"xv_{k}", inp["xa_wv"][l][k * 128:(k + 1) * 128, :])
    for c in range(8):
        put(f"xo_{c}", inp["xa_wo"][l][c * 128:(c + 1) * 128, :])
    put("wst", np.concatenate([inp["sgu_w"][l][g].T for g in range(4)], axis=1))
    return out


PCOLS = {}
_pc = 0
for _l in range(DEPTH):
    for _n in ("ln1_g", "ln1_b", "ln2_g", "ln2_b", "ln3_g", "ln3_b", "ln4_g", "ln4_b"):
        PCOLS[(_l, _n)] = _pc
        _pc += 8
    PCOLS[(_l, "subln")] = _pc
    _pc += 1
PCOLS["invf"] = _pc
_pc += 1
PCOLS["sign"] = _pc
_pc += 1
NPAR = _pc


def pack_params(inp):
    P = np.zeros((128, NPAR), np.float32)
    for l in range(DEPTH):
        for n in ("ln1_g", "ln1_b", "ln2_g", "ln2_b", "ln3_g", "ln3_b", "ln4_g", "ln4_b"):
            P[:, PCOLS[(l, n)]:PCOLS[(l, n)] + 8] = inp[n][l].reshape(8, 128).T
        P[:, PCOLS[(l, "subln")]] = inp["diff_subln_g"][l]
    invf = (np.float32(1.0) / (np.float32(10000.0) ** (np.arange(0, 64, 2, dtype=np.float32) / np.float32(64)))).astype(np.float32)
    P[:, PCOLS["invf"]] = np.tile(invf, 4)
    P[:, PCOLS["sign"]] = np.tile(np.concatenate([np.ones(32, np.float32), -np.ones(32, np.float32)]), 2)
    return P


def pack_bcast(inp):
    out = np.zeros((DEPTH, 128, 1280), np.float32)
    for l in range(DEPTH):
        row = np.concatenate([inp["sgu_ln_g"][l], inp["sgu_ln_b"][l], inp["diff_lq1"][l], inp["diff_lk1"][l],
                              inp["diff_lq2"][l], inp["diff_lk2"][l], inp["sgu_b"][l].reshape(-1)])
        out[l] = np.broadcast_to(row[None, :], (128, 1280))
    return out


def pack_consts():
    c = np.zeros((128, 512 + 128 + 1024), np.float32)
    k = np.arange(128)[:, None]
    q = np.arange(256)[None, :]
    for j in range(2):
        c[:, j * 256:(j + 1) * 256] = ((128 * j + k) <= q).astype(np.float32)
    c[:, 512:640] = np.eye(128, dtype=np.float32)
    ind = np.zeros((8, 8, 128), np.float32)
    for n in range(8):
        ind[n, n, :] = 1.0
    c[:8, 640:1664] = ind.reshape(8, 1024)
    return c


class Tile:
    __slots__ = ("ap", "w", "rc", "rd", "name")

    def __init__(self, ap, name=""):
        self.ap = ap
        self.w = None
        self.rc = {}
        self.rd = []
        self.name = name


class Sub:
    __slots__ = ("ap", "parent")

    def __init__(self, ap, parent):
        self.ap = ap
        self.parent = parent


class Op:
    __slots__ = ("eng", "fn", "deps", "idx", "need_inc", "dma", "sem", "target", "waits", "tick")

    def __init__(self, eng, fn, dma):
        self.eng = eng
        self.fn = fn
        self.dma = dma
        self.deps = []
        self.need_inc = False
        self.sem = None
        self.target = 0
        self.waits = []
        self.tick = 0


ENGS = ("pe", "act", "dve", "pool", "sp")


class Prog:
    def __init__(self, nring=24):
        self.ops = {e: [] for e in ENGS}
        self.nring = nring
        self.ring_last = [None] * nring
        self.ring_cnt = [0] * nring
        self.ndma = 0
        self.npool = 0
        self.nsp = 0
        self.tiles = []
        self.pending_dma = []

    def tile(self, ap, name=""):
        t = Tile(ap, name)
        self.tiles.append(t)
        return t

    def op(self, eng, fn, reads=(), writes=(), dma=False, extra=()):
        o = Op(eng, fn, dma)
        deps = []
        writes = [t.parent if isinstance(t, Sub) else t for t in writes] + [t.parent for t in reads if isinstance(t, Sub)]
        reads = [t for t in reads if not isinstance(t, Sub)]
        for t in reads:
            if t.w is not None:
                deps.append(t.w)
        for t in writes:
            if t.w is not None:
                deps.append(t.w)
            deps.extend(t.rc.values())
            deps.extend(t.rd)
        deps.extend(extra)
        if dma:
            half = self.nring // 2
            if eng == "pool":
                r = half + (self.npool % (self.nring - half))
                self.npool += 1
            else:
                r = self.nsp % half
                self.nsp += 1
            self.ndma += 1
            if self.ring_last[r] is not None:
                deps.append(self.ring_last[r])
            self.ring_cnt[r] += 1
            o.sem = r
            o.target = 16 * self.ring_cnt[r]
            self.ring_last[r] = o
            self.pending_dma.append(o)
        o.deps = [d for d in deps if d is not o]
        for t in reads:
            if dma:
                t.rd.append(o)
            else:
                t.rc[eng] = o
        for t in writes:
            t.w = o
            t.rc = {}
            t.rd = []
        o.idx = len(self.ops[eng])
        self.ops[eng].append(o)
        return o

    def barrier(self):
        lasts = [self.ops[e][-1] for e in ENGS if self.ops[e]]
        pend = list(self.pending_dma)
        for e in ENGS:
            self.op(e, lambda en: en.nop(), extra=lasts + pend)
        self.pending_dma = []
        for t in self.tiles:
            t.w = None
            t.rc = {}
            t.rd = []

    def finalize(self):
        for e in ENGS:
            seen = {a: -1 for a in ENGS}
            seen_d = {}
            for o in self.ops[e]:
                for d in o.deps:
                    if d.dma:
                        if seen_d.get(d.sem, 0) >= d.target:
                            continue
                        seen_d[d.sem] = d.target
                        o.waits.append(("d", d.sem, d.target))
                    else:
                        if d.eng == e and e == "pe":
                            continue
                        if seen[d.eng] >= d.idx:
                            continue
                        seen[d.eng] = d.idx
                        d.need_inc = True
                        o.waits.append(("c", d.eng, d))
        for e in ENGS:
            n = 0
            for o in self.ops[e]:
                if o.need_inc and not o.dma:
                    n += 1
                    o.tick = n

    def emit(self, nc, esems, dsems):
        handles = {"pe": "tensor", "act": "scalar", "dve": "vector", "pool": "gpsimd", "sp": "sync"}
        with nc.Block() as block:
            for e in ENGS:
                def body(en, e=e):
                    for o in self.ops[e]:
                        for w in o.waits:
                            if w[0] == "d":
                                en.wait_ge(dsems[w[1]], w[2])
                            else:
                                en.wait_ge(esems[w[1]], w[2].tick)
                        ins = o.fn(en)
                        if o.dma:
                            ins.then_inc(dsems[o.sem], 16)
                        elif o.need_inc:
                            ins.then_inc(esems[e], 1)
                getattr(block, handles[e])(body)


class Pool:
    def __init__(self, tiles):
        self.t = tiles
        self.i = 0

    def get(self):
        t = self.t[self.i % len(self.t)]
        self.i += 1
        return t


import os
DBG = set(os.environ.get("KDBG", "").split(","))


def build(stages, n_layers=DEPTH):
    nc = bass.Bass("TRN2", target_bir_lowering=False)
    xT = nc.dram_tensor("xT", [SPC, 128, 8, SEQ], F32, kind="ExternalInput").ap()
    memT = nc.dram_tensor("memT", [SPC, 128, 8, MEM], F32, kind="ExternalInput").ap()
    posb = nc.dram_tensor("posb", [SPC, 128, SEQ], I32, kind="ExternalInput").ap()
    wpack = nc.dram_tensor("wpack", [DEPTH, 128, LAY.tot], F32, kind="ExternalInput").ap()
    params = nc.dram_tensor("params", [128, NPAR], F32, kind="ExternalInput").ap()
    bcast = nc.dram_tensor("bcast", [DEPTH, 128, 1280], F32, kind="ExternalInput").ap()
    consts = nc.dram_tensor("consts", [128, 1664], F32, kind="ExternalInput").ap()
    outT = nc.dram_tensor("outT", [SPC, 128, 8, SEQ], F32, kind="ExternalOutput").ap()

    P = Prog()
    import contextlib
    es = contextlib.ExitStack()

    def sb(name, shape, dt):
        return es.enter_context(nc.sbuf_tensor(name, shape, dt))

    def ps(name, shape, dt):
        return es.enter_context(nc.psum_tensor(name, shape, dt))

    with es:
        XH = sb("XH", [128, 8, SEQ], BF16)
        XL = sb("XL", [128, 8, SEQ], BF16)
        BIGT = sb("BIGT", [128, 16384], F32)
        WP = sb("WP", [128, NSLOT, SLOT], BF16)
        PAR = sb("PAR", [128, NPAR], F32)
        BC = sb("BC", [128, 1280], F32)
        MASK = sb("MASK", [128, 2, 256], BF16)
        IDENT = sb("IDENT", [128, 128], BF16)
        IND = sb("IND", [128, 8, 128], BF16)
        ONESB = sb("ONESB", [128, 128], BF16)
        ONESF = sb("ONESF", [128, 128], F32)
        WST = sb("WST", [128, 4, 128], BF16)
        SGB = sb("SGB", [128, 512], BF16)
        LAM = sb("LAM", [128, 8], F32)
        TF = sb("TF", [128, 8, NB], F32)
        TB = sb("TB", [128, 8, NB], BF16)
        STG = sb("STG", [128, 1, 8, NB], F32)
        SCR = sb("SCR", [128, 8768], BF16)
        DED = sb("DED", [128, 4, NB], F32)
        KBAR = sb("KBAR", [128, 2, 8], BF16)
        KBF = sb("KBF", [128, 2, 8], F32)
        SMALL = sb("SMALL", [128, 64], F32)
        PSF = [ps(f"PSF{i}", [128, 2, NB], F32) for i in range(8)]
        PSH = PSF[7][:, :, :].bitcast(BF16)

        xh_t = [[P.tile(None, f"xh{b}_{m}") for m in range(8)] for b in range(NBLK)]
        wslots = Pool([P.tile(WP[:, i, :], f"w{i}") for i in range(NSLOT)])
        bank_t = [P.tile(None, f"bank{i}") for i in range(8)]
        PSA = Pool([Sub(PSF[bk][:, h, :], bank_t[bk]) for h in range(2) for bk in (0, 1, 2)])
        PSB = Pool([Sub(PSF[bk][:, h, :], bank_t[bk]) for h in range(2) for bk in (3, 4, 5, 6)])
        PSC = Pool([Sub(PSF[7][:, h, :], bank_t[7]) for h in range(2)])
        psh_t = bank_t[7]
        PSA2 = Pool([Sub(PSF[bk][:, :, :], bank_t[bk]) for bk in (0, 1, 2)])
        PSB2 = Pool([Sub(PSF[bk][:, :, :], bank_t[bk]) for bk in (3, 4, 5, 6)])
        tf = Pool([P.tile(TF[:, i, :], f"tf{i}") for i in range(8)])
        tb = Pool([P.tile(TB[:, i, :], f"tb{i}") for i in range(8)])
        tb2 = Pool([P.tile(TB[:, 2 * i:2 * i + 2, :], f"tbp{i}") for i in range(4)])
        stg = Pool([P.tile(STG[:, i], f"stg{i}") for i in range(1)])
        ded = [P.tile(DED[:, i, :], f"ded{i}") for i in range(4)]
        print("sbuf bytes remaining", nc.sbuf_bytes_remaining)
        par_t = P.tile(PAR, "par")
        bc_t = P.tile(BC, "bc")
        const_t = P.tile(None, "const")
        lam_t = P.tile(LAM, "lam")
        small_t = P.tile(SMALL, "small")
        acc_t = [[P.tile(None, f"acc{b}_{m}") for m in range(8)] for b in range(NBLK)]
        ka_t = [[P.tile(None, f"ka{b}_{c}") for c in range(4)] for b in range(NBLK)]
        kc_t = [[P.tile(None, f"kc{b}_{c}") for c in range(2)] for b in range(NBLK)]
        v_t = [[[P.tile(None, f"v{b}_{t}_{p}") for p in range(3)] for t in range(2)] for b in range(NBLK)]
        zz_t = [P.tile(None, f"zz{m}") for m in range(8)]
        zz2_t = [P.tile(None, f"zzb{m}") for m in range(8)]
        scr_t = {}

        def scr(name):
            if name not in scr_t:
                scr_t[name] = P.tile(None, "scr_" + name)
            return scr_t[name]

        BIGB = BIGT[:, :].bitcast(BF16)
        ACC = BIGT[:, :].rearrange("p (c t) -> p c t", c=8)
        KA = BIGB[:, 0:8192].rearrange("p (c t) -> p c t", c=4)
        KC = BIGB[:, 8192:12288].rearrange("p (c t) -> p c t", c=2)
        VV = BIGB[:, 12288:24576].rearrange("p (t c) -> p t c", t=16)
        ZZ = BIGT[:, 12288:14336].rearrange("p (c t) -> p c t", c=8)
        ZZ2 = BIGT[:, 14336:16384].rearrange("p (c t) -> p c t", c=8)
        QA = SCR[:, 0:2048].rearrange("p (c t) -> p c t", c=4)
        QC = SCR[:, 2048:3072].rearrange("p (c t) -> p c t", c=2)
        UU = SCR[:, 3072:4096].rearrange("p (c t) -> p c t", c=4)
        VS = SCR[:, 4096:4608].rearrange("p (t c) -> p t c", t=2)
        MIX = SCR[:, 4608:7168].rearrange("p (c t) -> p c t", c=10)
        BIAST = SCR[:, 7680:8704].rearrange("p (h t) -> p h t", h=4)
        BIASQ = SCR[:, 8704:8768].rearrange("p (t h n) -> p t h n", t=2, h=4)
        QX = SCR[:, 0:2048].rearrange("p (c t) -> p c t", c=8)
        XO = SCR[:, 2048:4096].rearrange("p (c t) -> p c t", c=8)
        MEMB = BIGB[:, 0:2048].rearrange("p (c t) -> p c t", c=8)
        KM = BIGB[:, 2048:4096].rearrange("p (c t) -> p c t", c=8)
        VM = BIGB[:, 4096:6144].rearrange("p (t c) -> p t c", t=2)

        def dma_in(eng, out_ap, in_ap, wt, rt=()):
            return P.op(eng, lambda e: e.dma_start(out=out_ap, in_=in_ap), reads=rt, writes=wt, dma=True)

        def wload(l, name):
            o, s = LAY.off[name]
            t = wslots.get()
            dma_in("pool", t.ap[:, 0:s], wpack[l, :, o:o + s], [t])
            return t

        def mm(out_t, out_ap, lt, l_ap, rt, r_ap, start, stop):
            rd = [x for x in (lt, rt) if x is not None]
            if not start:
                rd.append(out_t)
            P.op("pe", lambda e: e.matmul(out_ap, l_ap, r_ap, start=start, stop=stop), reads=rd, writes=[out_t])

        def act(out_t, out_ap, in_ts, in_ap, func, bias=0.0, scale=1.0, extra_w=()):
            P.op("act", lambda e: e.activation(out_ap, in_ap, func, bias=bias, scale=scale),
                 reads=in_ts, writes=[out_t] + list(extra_w))

        def tt(eng, out_t, out_ap, rts, a_ap, b_ap, op):
            P.op(eng, lambda e: e.tensor_tensor(out_ap, a_ap, b_ap, op), reads=rts, writes=[out_t])

        def stt(eng, out_t, out_ap, rts, a_ap, scalar, b_ap, op0, op1):
            P.op(eng, lambda e: e.scalar_tensor_tensor(out_ap, a_ap, scalar, b_ap, op0, op1), reads=rts, writes=[out_t])

        def ts(eng, out_t, out_ap, rts, a_ap, s1, s2, op0, op1=None):
            if op1 is None:
                P.op(eng, lambda e: e.tensor_scalar(out_ap, a_ap, s1, None, op0), reads=rts, writes=[out_t])
            else:
                P.op(eng, lambda e: e.tensor_scalar(out_ap, a_ap, s1, s2, op0, op1), reads=rts, writes=[out_t])

        def rsqrt(out_t, out_ap, rts, in_ap, mul, add):
            ts("dve", out_t, out_ap, rts, in_ap, mul, add, ALU.mult, ALU.add)
            P.op("act", lambda e: e.activation(out_ap, out_ap, AF.Ln), reads=[out_t], writes=[out_t])
            P.op("act", lambda e: e.activation(out_ap, out_ap, AF.Exp, scale=-0.5), reads=[out_t], writes=[out_t])

        def recip(out_t, out_ap, in_t, in_ap):
            P.op("act", lambda e: e.activation(out_ap, in_ap, AF.Ln), reads=[in_t], writes=[out_t])
            P.op("act", lambda e: e.activation(out_ap, out_ap, AF.Exp, scale=-1.0), reads=[out_t], writes=[out_t])

        def cp(eng, out_t, out_ap, rts, in_ap):
            if eng == "act":
                P.op("act", lambda e: e.copy(out_ap, in_ap), reads=rts, writes=[out_t])
            else:
                P.op(eng, lambda e: e.tensor_copy(out_ap, in_ap), reads=rts, writes=[out_t])

        def bs(b):
            return slice(b * NB, (b + 1) * NB)

        def pcol(key, m=0):
            c = PCOLS[key] + m
            return PAR[:, c:c + 1]

        dma_in("sp", PAR[:, :], params[:, :], [par_t])
        dma_in("pool", MASK[:, :, :], consts[:, 0:512].rearrange("p (j t) -> p j t", j=2), [const_t])
        dma_in("pool", IDENT[:, :], consts[:, 512:640], [const_t])
        dma_in("pool", IND[0:8, :, :], consts[0:8, 640:1664].rearrange("p (b t) -> p b t", b=8), [const_t])
        P.op("dve", lambda e: e.memset(ONESB[:, :], 1.0), writes=[const_t])
        P.op("dve", lambda e: e.memset(ONESF[:, :], 1.0), writes=[const_t])

        def split_hilo(b, m, r_t, r_ap):
            cp("act", xh_t[b][m], XH[:, m, bs(b)], [r_t], r_ap)
            tt("dve", xh_t[b][m], XL[:, m, bs(b)], [r_t, xh_t[b][m]], r_ap, XH[:, m, bs(b)], ALU.subtract)

        def load_x(s):
            for b in range(NBLK):
                st = stg.get()
                dma_in("sp", st.ap, xT[s, :, :, bs(b)], [st])
                for m in range(8):
                    cp("act", xh_t[b][m], XH[:, m, bs(b)], [st], st.ap[:, m, :])
                for m in range(8):
                    tt("dve", xh_t[b][m], XL[:, m, bs(b)], [st, xh_t[b][m]], st.ap[:, m, :], XH[:, m, bs(b)], ALU.subtract)

        def store_x(s):
            for b in range(NBLK):
                st = stg.get()
                for m in range(8):
                    tt("dve", st, st.ap[:, m, :], [xh_t[b][m]], XH[:, m, bs(b)], XL[:, m, bs(b)], ALU.add)
                P.op("sp", lambda e, st=st, b=b: e.dma_start(out=outT[s, :, :, bs(b)], in_=st.ap), reads=[st], dma=True)

        ln_sums = {}

        def ln_A(l, which, b, ysrc, zt, z_ap, dset=0):
            ln_A1(l, which, b, ysrc, zt, z_ap, dset)
            ln_A2(l, which, b, zt, z_ap, dset)

        def ln_A1(l, which, b, ysrc, zt, z_ap, dset=0):
            s1 = PSC.get()
            s2 = PSC.get()
            ln_sums[dset] = (s1, s2)
            for m in range(8):
                yt, yap = ysrc(m)
                stt("dve", zt[m], z_ap[:, m, :], [xh_t[b][m], yt], XH[:, m, bs(b)], ALPHA, yap, ALU.mult, ALU.add)
            for m in range(8):
                stt("dve", zt[m], z_ap[:, m, :], [xh_t[b][m], zt[m]], XL[:, m, bs(b)], ALPHA, z_ap[:, m, :], ALU.mult, ALU.add)
                mm(s1, s1.ap, const_t, ONESF[:, :], zt[m], z_ap[:, m, :], m == 0, m == 7)
            for m in range(8):
                q = tf.get()
                act(q, q.ap, [zt[m]], z_ap[:, m, :], AF.Square)
                mm(s2, s2.ap, const_t, ONESF[:, :], q, q.ap, m == 0, m == 7)

        def ln_A2(l, which, b, zt, z_ap, dset=0):
            s1, s2 = ln_sums[dset]
            mean = ded[2 * dset]
            ts("dve", mean, mean.ap, [s1], s1.ap, 1.0 / D, None, ALU.mult)
            msq = tf.get()
            tt("dve", msq, msq.ap, [mean], mean.ap, mean.ap, ALU.mult)
            var = tf.get()
            stt("dve", var, var.ap, [s2, msq], s2.ap, 1.0 / D, msq.ap, ALU.mult, ALU.subtract)
            rstd = ded[2 * dset + 1]
            rsqrt(rstd, rstd.ap, [var], var.ap, 1.0, EPS)
            tt("dve", mean, mean.ap, [mean, rstd], mean.ap, rstd.ap, ALU.mult)

        def ln_B(l, which, b, zt, z_ap, dset=0):
            mean = ded[2 * dset]
            rstd = ded[2 * dset + 1]
            for m in range(8):
                tt("dve", zt[m], z_ap[:, m, :], [zt[m], rstd], z_ap[:, m, :], rstd.ap, ALU.mult)
            for m in range(8):
                tt("dve", zt[m], z_ap[:, m, :], [zt[m], mean], z_ap[:, m, :], mean.ap, ALU.subtract)
            for m in range(8):
                act(xh_t[b][m], XH[:, m, bs(b)], [zt[m], par_t], z_ap[:, m, :], AF.Identity,
                    bias=pcol((l, which + "_b"), m), scale=pcol((l, which + "_g"), m))
            for m in range(8):
                act(zt[m], z_ap[:, m, :], [zt[m], par_t], z_ap[:, m, :], AF.Identity,
                    bias=pcol((l, which + "_b"), m), scale=pcol((l, which + "_g"), m))
            for m in range(8):
                tt("dve", xh_t[b][m], XL[:, m, bs(b)], [zt[m], xh_t[b][m]], z_ap[:, m, :], XH[:, m, bs(b)], ALU.subtract)

        def ln_block(l, which, b, ysrc, zt, z_ap):
            ln_A(l, which, b, ysrc, zt, z_ap, 0)
            ln_B(l, which, b, zt, z_ap, 0)

        def ffn(l, f, which):
            rem = NJ % JG
            groups = ([list(range(0, rem))] if rem else []) + [list(range(j, j + JG)) for j in range(rem, NJ, JG)]

            def lnA1(b):
                ln_A1(l, which, b, lambda m, b=b: (acc_t[b][m], ACC[:, m, bs(b)]), acc_t[b], ACC[:, :, bs(b)], b % 2)

            def lnA2(b):
                ln_A2(l, which, b, acc_t[b], ACC[:, :, bs(b)], b % 2)

            def lnB(b):
                ln_B(l, which, b, acc_t[b], ACC[:, :, bs(b)], b % 2)
            if "g1" in DBG:
                groups = groups[:1]
            for gi, grp in enumerate(groups):
                wg = {j: wload(l, f"g{f}_{j}") for j in grp}
                wu = {j: wload(l, f"u{f}_{j}") for j in grp}
                wd = {j: wload(l, f"d{f}_{j}") for j in grp}
                for b in range(NBLK if "nomm" not in DBG else 0):
                    hs = {}
                    for j in grp:
                        pg = PSA.get()
                        pu = PSA.get()
                        for k in range(8):
                            mm(pg, pg.ap, wg[j], wg[j].ap[:, k * 128:(k + 1) * 128], xh_t[b][k], XH[:, k, bs(b)], k == 0, k == 7)
                        for k in range(8):
                            mm(pu, pu.ap, wu[j], wu[j].ap[:, k * 128:(k + 1) * 128], xh_t[b][k], XH[:, k, bs(b)], k == 0, k == 7)
                        if "nocons" in DBG:
                            continue
                        sg = tf.get()
                        if "nosilu" in DBG:
                            cp("act", sg, sg.ap, [pg], pg.ap)
                        else:
                            act(sg, sg.ap, [pg], pg.ap, AF.Silu)
                        h = tb.get()
                        if "nott" in DBG:
                            cp("dve", h, h.ap, [sg], sg.ap)
                            cp("dve", h, h.ap, [pu], pu.ap)
                        else:
                            tt("dve", h, h.ap, [sg, pu], sg.ap, pu.ap, ALU.mult)
                        hs[j] = h
                    pd = [PSB.get() for _ in range(8)]
                    for m in range(8 if "nodown" not in DBG else 0):
                        for ji, j in enumerate(grp):
                            mm(pd[m], pd[m].ap, wd[j], wd[j].ap[:, m * 128:(m + 1) * 128], hs[j], hs[j].ap, ji == 0, ji == len(grp) - 1)
                    for m in range(8):
                        if gi == 0:
                            act(acc_t[b][m], ACC[:, m, bs(b)], [pd[m]], pd[m].ap, AF.Copy, scale=0.5)
                        else:
                            stt("dve", acc_t[b][m], ACC[:, m, bs(b)], [pd[m], acc_t[b][m]], pd[m].ap, 0.5, ACC[:, m, bs(b)], ALU.mult, ALU.add)
                    if gi == len(groups) - 1:
                        if b >= 1:
                            lnA1(b - 1)
                        if b >= 2:
                            lnB(b - 2)
                        if b >= 1:
                            lnA2(b - 1)
            lnA1(NBLK - 1)
            lnB(NBLK - 2)
            lnA2(NBLK - 1)
            lnB(NBLK - 1)

        def rope_tables(s, b, dset=1):
            pi_t = tf.get()
            P.op("sp", lambda e: e.dma_start(out=pi_t.ap.bitcast(I32), in_=posb[s, :, bs(b)]), writes=[pi_t], dma=True)
            ang = tf.get()
            cp("dve", ang, ang.ap, [pi_t], pi_t.ap.bitcast(I32))
            ts("dve", ang, ang.ap, [ang, par_t], ang.ap, pcol("invf"), None, ALU.mult)
            C1 = 6.28125
            C2 = 2.0 * math.pi - C1

            def reduce(add):
                t = tf.get()
                ki = tf.get()
                if add:
                    ts("dve", t, t.ap, [ang], ang.ap, 1.0 / (2.0 * math.pi), add / (2.0 * math.pi), ALU.mult, ALU.add)
                else:
                    ts("dve", t, t.ap, [ang], ang.ap, 1.0 / (2.0 * math.pi), None, ALU.mult)
                cp("dve", ki, ki.ap.bitcast(I32), [t], t.ap)
                cp("dve", t, t.ap, [ki], ki.ap.bitcast(I32))
                if add:
                    stt("dve", ki, ki.ap, [t, ang], t.ap, -C1, ang.ap, ALU.mult, ALU.add)
                    ts("dve", ki, ki.ap, [ki], ki.ap, add, None, ALU.add)
                else:
                    stt("dve", ki, ki.ap, [t, ang], t.ap, -C1, ang.ap, ALU.mult, ALU.add)
                stt("dve", ki, ki.ap, [t, ki], t.ap, -C2, ki.ap, ALU.mult, ALU.add)
                ts("dve", ki, ki.ap, [ki], ki.ap, -math.pi, None, ALU.max)
                ts("dve", ki, ki.ap, [ki], ki.ap, math.pi, None, ALU.min)
                return ki
            sn = ded[2 * dset + 1]
            cs = ded[2 * dset]
            r1 = reduce(0.0)
            act(sn, sn.ap, [r1], r1.ap, AF.Sin)
            r2 = reduce(0.5 * math.pi)
            act(cs, cs.ap, [r2], r2.ap, AF.Sin)
            ts("dve", sn, sn.ap, [sn, par_t], sn.ap, pcol("sign"), None, ALU.mult)
            ts("dve", sn, sn.ap, [sn], sn.ap, -1.0, None, ALU.mult)
            return cs, sn

        def proj_rope(l, wa, was, b, cs, sn, out_t, out_ap, bd=False):
            p1 = PSA.get()
            p2 = PSA.get()
            for k in range(8):
                mm(p1, p1.ap, wa, wa.ap[:, k * 128:(k + 1) * 128], xh_t[b][k], XH[:, k, bs(b)], k == 0, k == 7)
            for k in range(8):
                mm(p2, p2.ap, was, was.ap[:, k * 128:(k + 1) * 128], xh_t[b][k], XH[:, k, bs(b)], k == 0, k == 7)
            t1 = tf.get()
            tt("dve", t1, t1.ap, [p1, cs], p1.ap, cs.ap, ALU.mult)
            t2 = tf.get()
            tt("dve", t2, t2.ap, [p2, sn], p2.ap, sn.ap, ALU.mult)
            if bd:
                tt("dve", out_t, out_ap[0:64, 0:NB], [t1, t2], t1.ap[0:64, :], t2.ap[0:64, :], ALU.add)
                tt("dve", out_t, out_ap[64:128, NB:2 * NB], [t1, t2], t1.ap[64:128, :], t2.ap[64:128, :], ALU.add)
            else:
                tt("dve", out_t, out_ap, [t1, t2], t1.ap, t2.ap, ALU.add)

        def mixer_consts(l):
            P.op("dve", lambda e: e.memset(SCR[:, 0:3072], 0.0), writes=[scr("qa"), scr("qc")])
            dma_in("sp", BC[:, :], bcast[l, :, :], [bc_t])
            o, s_ = LAY.off["wst"]
            wt = wslots.get()
            dma_in("pool", wt.ap[:, 0:512], wpack[l, :, o:o + 512], [wt])
            for g in range(4):
                tt("dve", const_t, WST[:, g, :], [wt, const_t], wt.ap[:, g * 128:(g + 1) * 128], MASK[:, 0, 0:128], ALU.mult)
            cp("dve", const_t, SGB[:, :], [bc_t], BC[:, 768:1280])
            lam_init = 0.8 - 0.6 * math.exp(-0.3 * l)
            t = tf.get()
            tt("dve", t, t.ap[:, 0:64], [bc_t], BC[:, 512:576], BC[:, 576:640], ALU.mult)
            tt("dve", t, t.ap[:, 64:128], [bc_t], BC[:, 640:704], BC[:, 704:768], ALU.mult)
            P.op("dve", lambda e: e.reduce_sum(LAM[:, 0:1], t.ap[:, 0:64], AX.X), reads=[t], writes=[lam_t])
            P.op("dve", lambda e: e.reduce_sum(LAM[:, 1:2], t.ap[:, 64:128], AX.X), reads=[t], writes=[lam_t])
            act(lam_t, LAM[:, 2:4], [lam_t], LAM[:, 0:2], AF.Exp)
            tt("dve", lam_t, LAM[:, 4:5], [lam_t], LAM[:, 3:4], LAM[:, 2:3], ALU.subtract)
            ts("dve", lam_t, LAM[:, 4:5], [lam_t], LAM[:, 4:5], -lam_init, None, ALU.add)
            ts("dve", lam_t, LAM[:, 5:6], [par_t], pcol((l, "subln")), 1.0 - lam_init, None, ALU.mult)

        def mixer_phase1(l, s):
            wv = [wload(l, f"wv_{k}") for k in range(8)]
            for b in range(NBLK):
                for tI in range(2):
                    tok = slice(b * NB + tI * 128, b * NB + (tI + 1) * 128)
                    pa = PSA.get()
                    pb = PSA.get()
                    for k in range(8):
                        mm(pa, pa.ap, xh_t[b][k], XH[:, k, tok], wv[k], wv[k].ap[:, 0:256], k == 0, k == 7)
                    for k in range(8):
                        mm(pb, pb.ap, xh_t[b][k], XH[:, k, tok], wv[k], wv[k].ap[:, 256:512], k == 0, k == 7)
                    cp("act", v_t[b][tI][0], VV[:, 2 * b + tI, 0:256], [pa], pa.ap)
                    cp("act", v_t[b][tI][1], VV[:, 2 * b + tI, 256:512], [pb], pb.ap)
                    pc = PSA.get()
                    for k in range(8):
                        mm(pc, pc.ap, xh_t[b][k], XH[:, k, tok], wv[k], wv[k].ap[:, 512:768], k == 0, k == 7)
                    cp("act", v_t[b][tI][2], VV[:, 2 * b + tI, 512:768], [pc], pc.ap)
            kw = []
            for c in range(6):
                if c < 4:
                    kw.append((wload(l, f"ka_{c}"), wload(l, f"kas_{c}")))
                else:
                    kw.append((wload(l, f"kc_{c - 4}"), wload(l, f"kcs_{c - 4}")))
            tabs = {0: rope_tables(s, 0, 0)}
            for b in range(NBLK):
                if b + 1 < NBLK:
                    tabs[b + 1] = rope_tables(s, b + 1, (b + 1) % 2)
                cs, sn = tabs[b]
                for c in range(6):
                    dst = KA[:, c, bs(b)] if c < 4 else KC[:, c - 4, bs(b)]
                    proj_rope(l, kw[c][0], kw[c][1], b, cs, sn, ka_t[b][c] if c < 4 else kc_t[b][c - 4], dst)
            for c in range(2):
                for b in range(NBLK):
                    P.op("dve", lambda e, c=c, b=b: e.reduce_sum(KBF[:, c, b:b + 1], KC[:, c, bs(b)], AX.X),
                         reads=[kc_t[b][c]], writes=[small_t])
            ts("dve", small_t, KBAR[:, :, :], [small_t], KBF[:, :, :], 1.0 / 256.0, None, ALU.mult)

        def run_streams(streams, nkt):
            pend = [sc(0) for sc, _ in streams]
            for kt in range(nkt):
                for i, (sc, fin) in enumerate(streams):
                    nx = sc(kt + 1) if kt + 1 < nkt else None
                    fin(kt, pend[i])
                    pend[i] = nx

        def mixer_phase2(l, s):
            lam_init = 0.8 - 0.6 * math.exp(-0.3 * l)
            def front(b):
                cs, sn = rope_tables(s, b)
                for c in range(4):
                    wa, was = wload(l, f"qa_{c}"), wload(l, f"qas_{c}")
                    proj_rope(l, wa, was, b, cs, sn, scr("qa"), QA[:, c, :], bd=True)
                for c in range(2):
                    wa, was = wload(l, f"qc_{c}"), wload(l, f"qcs_{c}")
                    proj_rope(l, wa, was, b, cs, sn, scr("qc"), QC[:, c, :], bd=True)
                for g in range(4 if "nosgu" not in DBG else 0):
                    w = wload(l, f"u_{g}")
                    p = PSA.get()
                    for k in range(8):
                        mm(p, p.ap[0:64, :], w, w.ap[:, k * 64:(k + 1) * 64], xh_t[b][k], XH[:, k, bs(b)], k == 0, k == 7)
                    act(scr("u"), UU[0:64, g, :], [p], p.ap[0:64, :], AF.Gelu)
                wvs = [wload(l, f"vs_{h}") for h in range(2)]
                for tI in range(2 if "nosgu" not in DBG else 0):
                    tok = slice(b * NB + tI * 128, b * NB + (tI + 1) * 128)
                    p = PSA.get()
                    for k in range(8):
                        w = wvs[k // 4]
                        mm(p, p.ap, xh_t[b][k], XH[:, k, tok], w, w.ap[:, (k % 4) * 256:(k % 4 + 1) * 256], k == 0, k == 7)
                    gv = tf.get()
                    act(gv, gv.ap, [p], p.ap, AF.Gelu)
                    sq = tf.get()
                    P.op("dve", lambda e, gv=gv: e.reduce_sum(SMALL[:, 0:1], gv.ap, AX.X), reads=[gv], writes=[small_t])
                    act(sq, sq.ap, [gv], gv.ap, AF.Square)
                    P.op("dve", lambda e, sq=sq: e.reduce_sum(SMALL[:, 1:2], sq.ap, AX.X), reads=[sq], writes=[small_t])
                    ts("dve", small_t, SMALL[:, 2:3], [small_t], SMALL[:, 0:1], 1.0 / 256.0, None, ALU.mult)
                    tt("dve", small_t, SMALL[:, 3:4], [small_t], SMALL[:, 2:3], SMALL[:, 2:3], ALU.mult)
                    stt("dve", small_t, SMALL[:, 4:5], [small_t], SMALL[:, 1:2], 1.0 / 256.0, SMALL[:, 3:4], ALU.mult, ALU.subtract)
                    rsqrt(small_t, SMALL[:, 5:6], [small_t], SMALL[:, 4:5], 1.0, EPS)
                    ts("dve", gv, gv.ap, [gv, small_t], gv.ap, SMALL[:, 2:3], None, ALU.subtract)
                    ts("dve", gv, gv.ap, [gv, small_t], gv.ap, SMALL[:, 5:6], None, ALU.mult)
                    tt("dve", gv, gv.ap, [gv, bc_t], gv.ap, BC[:, 0:256], ALU.mult)
                    tt("dve", scr("vs"), VS[:, tI, :], [gv, bc_t], gv.ap, BC[:, 256:512], ALU.add)
                for tI in range(2 if "nogate" not in DBG else 0):
                    qblk = b
                    pg = PSA.get()
                    for h in range(4):
                        rows = slice((h % 2) * 64, (h % 2) * 64 + 64)
                        mm(pg, pg.ap[:, h * 8:(h + 1) * 8], scr("qc"), QC[rows, h // 2, (h % 2) * NB + tI * 128:(h % 2) * NB + (tI + 1) * 128],
                           small_t, KBAR[rows, h // 2, :], True, True)
                    gt = tf.get()
                    cp("dve", gt, gt.ap[:, 0:32], [pg], pg.ap[:, 0:32])
                    g3 = gt.ap[:, 0:32].rearrange("p (h n) -> p h n", h=4)
                    if qblk < 8:
                        P.op("dve", lambda e, g3=g3, qblk=qblk: e.memset(g3[:, :, qblk:8], -1e30), reads=[gt], writes=[gt])
                    for h in range(4):
                        P.op("dve", lambda e, gt=gt, h=h: e.max(gt.ap[:, 64 + h * 8:72 + h * 8], gt.ap[:, h * 8:(h + 1) * 8]),
                             reads=[gt], writes=[gt])
                    for h in range(4):
                        ts("dve", gt, gt.ap[:, 128 + h * 8:136 + h * 8], [gt], gt.ap[:, h * 8:(h + 1) * 8],
                           gt.ap[:, 64 + h * 8 + 2:64 + h * 8 + 3], None, ALU.is_ge)
                    ts("dve", gt, gt.ap[:, 160:192], [gt], gt.ap[:, 128:160], BIG, -BIG, ALU.mult, ALU.add)
                    b3 = gt.ap[:, 160:192].rearrange("p (h n) -> p h n", h=4)
                    P.op("dve", lambda e, b3=b3, qblk=qblk: e.memset(b3[:, :, qblk:8], 0.0), reads=[gt], writes=[gt])
                    cp("dve", scr("biasq"), BIASQ[:, tI, :, :], [gt], b3)
                for h in range(4 if "nogate" not in DBG else 0):
                    p = PSA.get()
                    for tI in range(2):
                        mm(p, p.ap[0:8, tI * 128:(tI + 1) * 128], scr("biasq"), BIASQ[:, tI, h, :], const_t, IDENT[:, :], True, True)
                    cp("dve", scr("biast"), BIAST[0:8, h, :], [p], p.ap[0:8, :])

            def mid(b):
                nkt = 2 * b + 2
                for g in range(4 if "nosgu" not in DBG else 0):
                    p = PSA.get()
                    for tI in range(2):
                        cols = slice(tI * 128, (tI + 1) * 128)
                        mm(p, p.ap[0:64, cols], scr("vs"), VS[:, tI, g * 64:(g + 1) * 64], const_t, WST[:, g, :], True, True)
                    sm = tf.get()
                    for tI in range(2):
                        cols = slice(tI * 128, (tI + 1) * 128)
                        tt("dve", sm, sm.ap[0:64, cols], [p, bc_t], p.ap[0:64, cols], BC[0:64, 768 + g * 128:768 + (g + 1) * 128], ALU.add)
                    tt("dve", scr("mix"), MIX[0:64, 4 + g, :], [scr("u"), sm], UU[0:64, g, :], sm.ap[0:64, :], ALU.mult)
                for h0 in range(0, 4 if "noA" not in DBG else 0, 2):
                    streams = []
                    accs = []
                    for h in (h0, h0 + 1):
                        obank = PSB2.get()
                        sbank = PSB2.get()
                        accs.append((h, obank, sbank))

                        def a_score(kt, h=h):
                            bank = PSA2.get()
                            mm(bank, bank.ap.rearrange("p a t -> p (a t)"), ka_t[kt // 2][h], KA[:, h, kt * 128:(kt + 1) * 128],
                               scr("qa"), QA[:, h, :], True, True)
                            return bank

                        def a_finish(kt, bank, h=h, obank=obank, sbank=sbank, b=b, nkt=nkt):
                            vt = v_t[kt // 2][kt % 2][h // 2]
                            pt = tb2.get()
                            act(pt, pt.ap, [bank], bank.ap, AF.Exp, scale=0.125)
                            if kt >= 2 * b:
                                for mp in range(2):
                                    tt("dve", pt, pt.ap[:, mp, :], [pt, const_t], pt.ap[:, mp, :], MASK[:, kt - 2 * b, :], ALU.mult)
                            p2d = pt.ap.rearrange("p a t -> p (a t)")
                            mm(obank, obank.ap.rearrange("p a t -> p (a t)"), vt, VV[:, kt, h * 128:(h + 1) * 128], pt, p2d, kt == 0, kt == nkt - 1)
                            mm(sbank, sbank.ap.rearrange("p a t -> p (a t)"), const_t, ONESB[:, :], pt, p2d, kt == 0, kt == nkt - 1)
                        streams.append((a_score, a_finish))
                    run_streams(streams, nkt)
                    for (h, obank, sbank) in accs:
                        po = [Sub(obank.ap[:, 0, :], obank.parent), Sub(sbank.ap[:, 0, :], sbank.parent),
                              Sub(obank.ap[:, 1, :], obank.parent), Sub(sbank.ap[:, 1, :], sbank.parent)]
                        r1 = tf.get()
                        recip(r1, r1.ap, po[1], po[1].ap)
                        r2 = tf.get()
                        recip(r2, r2.ap, po[3], po[3].ap)
                        tt("dve", r1, r1.ap, [r1, po[0]], r1.ap, po[0].ap, ALU.mult)
                        tt("dve", r2, r2.ap, [r2, po[2]], r2.ap, po[2].ap, ALU.mult)
                        o = tf.get()
                        stt("dve", o, o.ap, [r2, lam_t, r1], r2.ap, LAM[:, 4:5], r1.ap, ALU.mult, ALU.add)
                        sq = tf.get()
                        act(sq, sq.ap, [o], o.ap, AF.Square)
                        ss = PSC.get()
                        mm(ss, ss.ap, const_t, ONESF[:, :], sq, sq.ap, True, True)
                        rr = tf.get()
                        rsqrt(rr, rr.ap, [ss], ss.ap, 1.0 / 128.0, EPS)
                        tt("dve", o, o.ap, [o, rr], o.ap, rr.ap, ALU.mult)
                        ts("dve", scr("mix"), MIX[:, h, :], [o, lam_t], o.ap, LAM[:, 5:6], None, ALU.mult)
                streams = []
                accs = []
                for hp in range(2 if "noC" not in DBG else 0):
                    obank = PSB2.get()
                    sbank = PSB2.get()
                    accs.append((hp, obank, sbank))

                    def c_score(kt, hp=hp):
                        bank = PSA2.get()
                        o2d = bank.ap.rearrange("p a t -> p (a t)")
                        mm(bank, o2d, kc_t[kt // 2][hp], KC[:, hp, kt * 128:(kt + 1) * 128], scr("qc"), QC[:, hp, :], True, False)
                        mm(bank, o2d, const_t, IND[0:8, kt // 2, :], scr("biast"),
                           BIAST[0:8, 2 * hp:2 * hp + 2, :].rearrange("p h t -> p (h t)"), False, True)
                        return bank

                    def c_finish(kt, bank, hp=hp, obank=obank, sbank=sbank, b=b, nkt=nkt):
                        pt = tb2.get()
                        act(pt, pt.ap, [bank], bank.ap, AF.Exp, scale=0.125)
                        if kt >= 2 * b:
                            for hh in range(2):
                                tt("dve", pt, pt.ap[:, hh, :], [pt, const_t], pt.ap[:, hh, :], MASK[:, kt - 2 * b, :], ALU.mult)
                        vt = v_t[kt // 2][kt % 2][2]
                        p2d = pt.ap.rearrange("p a t -> p (a t)")
                        mm(obank, obank.ap.rearrange("p a t -> p (a t)"), vt, VV[:, kt, 512 + hp * 128:512 + (hp + 1) * 128], pt, p2d, kt == 0, kt == nkt - 1)
                        mm(sbank, sbank.ap.rearrange("p a t -> p (a t)"), const_t, ONESB[:, :], pt, p2d, kt == 0, kt == nkt - 1)
                    streams.append((c_score, c_finish))
                if streams:
                    run_streams(streams, nkt)
                for (hp, obank, sbank) in accs:
                    for hh in range(2):
                        rws = slice(hh * 64, hh * 64 + 64)
                        r1 = tf.get()
                        recip(r1, r1.ap[rws, :], sbank, sbank.ap[rws, hh, :])
                        tt("dve", scr("mix"), MIX[rws, 8 + hp, :], [r1, obank], r1.ap[rws, :], obank.ap[rws, hh, :], ALU.mult)

            def tail(b):
                wo = [wload(l, f"wo_{c}") for c in range(10)]
                ys = {}

                def ysrc(m, b=b, wo=wo):
                    p = PSA.get()
                    for c in range(10):
                        kk = 64 if 4 <= c < 8 else 128
                        mm(p, p.ap, wo[c], wo[c].ap[0:kk, m * 128:(m + 1) * 128], scr("mix"), MIX[0:kk, c, :], c == 0, c == 9)
                    return p, p.ap
                zt_, za_ = (zz_t, ZZ) if b % 2 == 0 else (zz2_t, ZZ2)
                ln_A1(l, "ln2", b, ysrc, zt_, za_, 0)
                if b >= 1:
                    pz, pa = (zz_t, ZZ) if (b - 1) % 2 == 0 else (zz2_t, ZZ2)
                    ln_B(l, "ln2", b - 1, pz, pa, 0)
                ln_A2(l, "ln2", b, zt_, za_, 0)


            front(0)
            mid(0)
            for b in range(NBLK):
                if b + 1 < NBLK:
                    front(b + 1)
                tail(b)
                if b + 1 < NBLK:
                    mid(b + 1)
            ln_B(l, "ln2", NBLK - 1, zz2_t, ZZ2, 0)

        def xattn(l, s):
            st = stg.get()
            dma_in("sp", st.ap[:, :, 0:MEM], memT[s, :, :, :], [st])
            cp("dve", scr("memb"), MEMB[:, :, :], [st], st.ap[:, :, 0:MEM])
            for m in range(8):
                w = wload(l, f"xk_{m}")
                p = PSA.get()
                for k in range(8):
                    mm(p, p.ap, w, w.ap[:, k * 128:(k + 1) * 128], scr("memb"), MEMB[:, k, :], k == 0, k == 7)
                cp("act", scr("km"), KM[:, m, :], [p], p.ap)
            wv = [wload(l, f"xv_{k}") for k in range(8)]
            for mt in range(2):
                for n in range(4):
                    p = PSA.get()
                    for k in range(8):
                        mm(p, p.ap, scr("memb"), MEMB[:, k, mt * 128:(mt + 1) * 128], wv[k], wv[k].ap[:, n * 256:(n + 1) * 256], k == 0, k == 7)
                    cp("act", scr("vm"), VM[:, mt, n * 256:(n + 1) * 256], [p], p.ap)
            def xq(b):
                for m in range(8):
                    w = wload(l, f"xq_{m}")
                    p = PSA.get()
                    for k in range(8):
                        mm(p, p.ap, w, w.ap[:, k * 128:(k + 1) * 128], xh_t[b][k], XH[:, k, bs(b)], k == 0, k == 7)
                    cp("act", scr("qx"), QX[:, m, :], [p], p.ap)

            def xatt(b):
                def x_score(h):
                    bank = PSA2.get()
                    for mt in range(2):
                        for c in range(2):
                            mm(bank, bank.ap[:, mt, :], scr("km"), KM[:, 2 * h + c, mt * 128:(mt + 1) * 128], scr("qx"), QX[:, 2 * h + c, :], c == 0, c == 1)
                    return bank

                def x_finish(h, bank):
                    pt = tb2.get()
                    act(pt, pt.ap, [bank], bank.ap, AF.Exp, scale=1.0 / 16.0)
                    pS = PSB.get()
                    for mt in range(2):
                        mm(pS, pS.ap, const_t, ONESB[:, :], pt, pt.ap[:, mt, :], mt == 0, mt == 1)
                    r1 = tf.get()
                    recip(r1, r1.ap, pS, pS.ap)
                    for c in range(2):
                        po = PSB.get()
                        for mt in range(2):
                            mm(po, po.ap, scr("vm"), VM[:, mt, (2 * h + c) * 128:(2 * h + c + 1) * 128], pt, pt.ap[:, mt, :], mt == 0, mt == 1)
                        tt("dve", scr("xo"), XO[:, 2 * h + c, :], [r1, po], r1.ap, po.ap, ALU.mult)
                pend = x_score(0)
                for h in range(4):
                    nxt = x_score(h + 1) if h < 3 else None
                    x_finish(h, pend)
                    pend = nxt

            def xtail(b):
                wo = [wload(l, f"xo_{c}") for c in range(8)]

                def ysrc(m, wo=wo):
                    p = PSA.get()
                    for c in range(8):
                        mm(p, p.ap, wo[c], wo[c].ap[:, m * 128:(m + 1) * 128], scr("xo"), XO[:, c, :], c == 0, c == 7)
                    return p, p.ap
                zt_, za_ = (zz_t, ZZ) if b % 2 == 0 else (zz2_t, ZZ2)
                ln_A1(l, "ln3", b, ysrc, zt_, za_, b % 2)
                if b >= 1:
                    pz, pa = (zz_t, ZZ) if (b - 1) % 2 == 0 else (zz2_t, ZZ2)
                    ln_B(l, "ln3", b - 1, pz, pa, (b - 1) % 2)
                ln_A2(l, "ln3", b, zt_, za_, b % 2)

            xq(0)
            for b in range(NBLK):
                xatt(b)
                if b + 1 < NBLK:
                    xq(b + 1)
                xtail(b)
            ln_B(l, "ln3", NBLK - 1, zz2_t, ZZ2, (NBLK - 1) % 2)

        for s in range(SPC):
            load_x(s)
            for (l, st_) in stages:
                P.barrier()
                if st_ == "ffn1":
                    ffn(l, 1, "ln1")
                elif st_ == "ffn2":
                    ffn(l, 2, "ln4")
                elif st_ == "mixer":
                    mixer_consts(l)
                    mixer_phase1(l, s)
                    if "p1only" not in DBG:
                        mixer_phase2(l, s)
                elif st_ == "xattn":
                    xattn(l, s)
            P.barrier()
            store_x(s)
        P.op("sp", lambda e: e.nop(), extra=list(P.pending_dma))

        P.finalize()
        import contextlib as _c
        ses = _c.ExitStack()
        with ses:
            esems = {e: ses.enter_context(nc.semaphore(f"es_{e}")) for e in ENGS}
            dsems = [ses.enter_context(nc.semaphore(f"ds_{i}")) for i in range(P.nring)]
            P.emit(nc, esems, dsems)
    return nc


FULL_STAGES = [(l, s) for l in range(DEPTH) for s in ("ffn1", "mixer", "xattn", "ffn2")]
_CACHE = {}


def kernel(stages=None, **inp):
    if stages is None:
        stages = FULL_STAGES
    inp = {k: np.asarray(v) for k, v in inp.items()}
    x = inp["x"].astype(np.float32, copy=False)
    mem = inp["mem"].astype(np.float32, copy=False)
    pos = inp["positions"].astype(np.int32, copy=False)
    xT = np.ascontiguousarray(x.reshape(BATCH, SEQ, 8, 128).transpose(0, 3, 2, 1))
    memT = np.ascontiguousarray(mem.reshape(BATCH, MEM, 8, 128).transpose(0, 3, 2, 1))
    posb = np.ascontiguousarray(np.broadcast_to(pos[:, None, :], (BATCH, 128, SEQ)))
    wpack = np.stack([pack_layer(inp, l) for l in range(DEPTH)], axis=0)
    params = pack_params(inp)
    bc = pack_bcast(inp)
    consts = pack_consts()
    key = tuple(stages)
    if key not in _CACHE:
        _CACHE[key] = build(list(stages))
    nc = _CACHE[key]
    in_maps = []
    for c in range(NCORES):
        sl = slice(c * SPC, (c + 1) * SPC)
        in_maps.append({"xT": xT[sl], "memT": memT[sl], "posb": posb[sl], "wpack": wpack,
                        "params": params, "bcast": bc, "consts": consts})
    res = run_bass_kernel_spmd(nc, in_maps, core_ids=list(range(NCORES)))
    outs = [np.asarray(r["outT"]) for r in res.results]
    oT = np.concatenate(outs, axis=0)
    out = oT.transpose(0, 3, 2, 1).reshape(BATCH, SEQ, D)
    return np.ascontiguousarray(out.astype(np.float32, copy=False))
```

```python
import math
import numpy as np
import concourse.bass as bass
import concourse.mybir as mybir
from concourse.bass_utils import run_bass_kernel_spmd

F32 = mybir.dt.float32
BF16 = mybir.dt.bfloat16
I32 = mybir.dt.int32
AF = mybir.ActivationFunctionType
ALU = mybir.AluOpType
AX = mybir.AxisListType

D = 1024
SEQ = 2048
BATCH = 16
DEPTH = 2
NCORES = 8
SPC = BATCH // NCORES
MEM = 256
DFF = 2816
NJ = DFF // 128
NB = 256
NBLK = SEQ // NB
ALPHA = (2.0 * DEPTH) ** 0.25
EPS = 1e-5
BIG = 30000.0
JG = 4
SLOT = 1024
NSLOT = 13


def _colpanel(W, c0, width):
    K = W.shape[0] // 128
    return W[:, c0:c0 + width].reshape(K, 128, width).transpose(1, 0, 2).reshape(128, K * width)


def _swap_cols(base):
    return list(range(base + 32, base + 64)) + list(range(base, base + 32))


class Layout:
    def __init__(self):
        self.off = {}
        self.tot = 0

    def add(self, name, size):
        self.off[name] = (self.tot, size)
        self.tot += size


def make_layout():
    L = Layout()
    for f in (1, 2):
        for j in range(NJ):
            L.add(f"g{f}_{j}", 1024)
            L.add(f"u{f}_{j}", 1024)
            L.add(f"d{f}_{j}", 1024)
    for c in range(4):
        L.add(f"ka_{c}", 1024)
        L.add(f"kas_{c}", 1024)
    for c in range(2):
        L.add(f"kc_{c}", 1024)
        L.add(f"kcs_{c}", 1024)
    for k in range(8):
        L.add(f"wv_{k}", 768)
    for c in range(4):
        L.add(f"qa_{c}", 1024)
        L.add(f"qas_{c}", 1024)
    for c in range(2):
        L.add(f"qc_{c}", 1024)
        L.add(f"qcs_{c}", 1024)
    for g in range(4):
        L.add(f"u_{g}", 512)
    for h in range(2):
        L.add(f"vs_{h}", 1024)
    for c in range(10):
        L.add(f"wo_{c}", 1024)
    for m in range(8):
        L.add(f"xq_{m}", 1024)
        L.add(f"xk_{m}", 1024)
    for k in range(8):
        L.add(f"xv_{k}", 1024)
    for c in range(8):
        L.add(f"xo_{c}", 1024)
    L.add("wst", 512)
    return L


LAY = make_layout()


def pack_layer(inp, l):
    out = np.zeros((128, LAY.tot), np.float32)

    def put(name, arr):
        o, s = LAY.off[name]
        assert arr.shape == (128, s), (name, arr.shape, s)
        out[:, o:o + s] = arr

    for f in (1, 2):
        wg, wu, wd = inp[f"ffn{f}_w_gate"][l], inp[f"ffn{f}_w_up"][l], inp[f"ffn{f}_w_down"][l]
        for j in range(NJ):
            put(f"g{f}_{j}", _colpanel(wg, j * 128, 128))
            put(f"u{f}_{j}", _colpanel(wu, j * 128, 128))
            put(f"d{f}_{j}", wd[j * 128:(j + 1) * 128, :])
    w_in = inp["mix_w_in"][l]
    swp = np.arange(w_in.shape[1])
    for base in list(range(0, 1024, 64)) + list(range(2048, 2560, 64)):
        swp[base:base + 64] = _swap_cols(base)
    w_sw = w_in[:, swp]
    for c in range(4):
        put(f"qa_{c}", _colpanel(w_in, c * 128, 128))
        put(f"qas_{c}", _colpanel(w_sw, c * 128, 128))
        put(f"ka_{c}", _colpanel(w_in, 512 + c * 128, 128))
        put(f"kas_{c}", _colpanel(w_sw, 512 + c * 128, 128))
    for c in range(2):
        put(f"qc_{c}", _colpanel(w_in, 2048 + c * 128, 128))
        put(f"qcs_{c}", _colpanel(w_sw, 2048 + c * 128, 128))
        put(f"kc_{c}", _colpanel(w_in, 2304 + c * 128, 128))
        put(f"kcs_{c}", _colpanel(w_sw, 2304 + c * 128, 128))
    wv = np.concatenate([w_in[:, 1024:1536], w_in[:, 2560:2816]], axis=1)
    for k in range(8):
        put(f"wv_{k}", wv[k * 128:(k + 1) * 128, :])
    for g in range(4):
        put(f"u_{g}", _colpanel(w_in, 1536 + g * 64, 64))
    wvs = w_in[:, 1792:2048]
    for h in range(2):
        put(f"vs_{h}", wvs[h * 512:(h + 1) * 512].reshape(4, 128, 256).transpose(1, 0, 2).reshape(128, 1024))
    w_out = inp["mix_w_out"][l]
    for c in range(4):
        put(f"wo_{c}", w_out[c * 128:(c + 1) * 128, :])
    for g in range(4):
        t = np.zeros((128, 1024), np.float32)
        t[:64] = w_out[512 + g * 64:512 + (g + 1) * 64, :]
        put(f"wo_{4 + g}", t)
    for hp in range(2):
        put(f"wo_{8 + hp}", w_out[768 + hp * 128:768 + (hp + 1) * 128, :])
    for m in range(8):
        put(f"xq_{m}", _colpanel(inp["xa_wq"][l], m * 128, 128))
        put(f"xk_{m}", _colpanel(inp["xa_wk"][l], m * 128, 128))
    for k in range(8):
        put(f"xv_{k}", inp["xa_wv"][l][k * 128:(k + 1) * 128, :])
    for c in range(8):
        put(f"xo_{c}", inp["xa_wo"][l][c * 128:(c + 1) * 128, :])
    put("wst", np.concatenate([inp["sgu_w"][l][g].T for g in range(4)], axis=1))
    return out


PCOLS = {}
_pc = 0
for _l in range(DEPTH):
    for _n in ("ln1_g", "ln1_b", "ln2_g", "ln2_b", "ln3_g", "ln3_b", "ln4_g", "ln4_b"):
        PCOLS[(_l, _n)] = _pc
        _pc += 8
    PCOLS[(_l, "subln")] = _pc
    _pc += 1
PCOLS["invf"] = _pc
_pc += 1
PCOLS["sign"] = _pc
_pc += 1
NPAR = _pc


def pack_params(inp):
    P = np.zeros((128, NPAR), np.float32)
    for l in range(DEPTH):
        for n in ("ln1_g", "ln1_b", "ln2_g", "ln2_b", "ln3_g", "ln3_b", "ln4_g", "ln4_b"):
            P[:, PCOLS[(l, n)]:PCOLS[(l, n)] + 8] = inp[n][l].reshape(8, 128).T
        P[:, PCOLS[(l, "subln")]] = inp["diff_subln_g"][l]
    invf = (np.float32(1.0) / (np.float32(10000.0) ** (np.arange(0, 64, 2, dtype=np.float32) / np.float32(64)))).astype(np.float32)
    P[:, PCOLS["invf"]] = np.tile(invf, 4)
    P[:, PCOLS["sign"]] = np.tile(np.concatenate([np.ones(32, np.float32), -np.ones(32, np.float32)]), 2)
    return P


def pack_bcast(inp):
    out = np.zeros((DEPTH, 128, 1280), np.float32)
    for l in range(DEPTH):
        row = np.concatenate([inp["sgu_ln_g"][l], inp["sgu_ln_b"][l], inp["diff_lq1"][l], inp["diff_lk1"][l],
                              inp["diff_lq2"][l], inp["diff_lk2"][l], inp["sgu_b"][l].reshape(-1)])
        out[l] = np.broadcast_to(row[None, :], (128, 1280))
    return out


def pack_consts():
    c = np.zeros((128, 512 + 128 + 1024), np.float32)
    k = np.arange(128)[:, None]
    q = np.arange(256)[None, :]
    for j in range(2):
        c[:, j * 256:(j + 1) * 256] = ((128 * j + k) <= q).astype(np.float32)
    c[:, 512:640] = np.eye(128, dtype=np.float32)
    ind = np.zeros((8, 8, 128), np.float32)
    for n in range(8):
        ind[n, n, :] = 1.0
    c[:8, 640:1664] = ind.reshape(8, 1024)
    return c


class Tile:
    __slots__ = ("ap", "w", "rc", "rd", "name")

    def __init__(self, ap, name=""):
        self.ap = ap
        self.w = None
        self.rc = {}
        self.rd = []
        self.name = name


class Sub:
    __slots__ = ("ap", "parent")

    def __init__(self, ap, parent):
        self.ap = ap
        self.parent = parent


class Op:
    __slots__ = ("eng", "fn", "deps", "idx", "need_inc", "dma", "sem", "target", "waits", "tick")

    def __init__(self, eng, fn, dma):
        self.eng = eng
        self.fn = fn
        self.dma = dma
        self.deps = []
        self.need_inc = False
        self.sem = None
        self.target = 0
        self.waits = []
        self.tick = 0


ENGS = ("pe", "act", "dve", "pool", "sp")


class Prog:
    def __init__(self, nring=24):
        self.ops = {e: [] for e in ENGS}
        self.nring = nring
        self.ring_last = [None] * nring
        self.ring_cnt = [0] * nring
        self.ndma = 0
        self.npool = 0
        self.nsp = 0
        self.tiles = []
        self.pending_dma = []

    def tile(self, ap, name=""):
        t = Tile(ap, name)
        self.tiles.append(t)
        return t

    def op(self, eng, fn, reads=(), writes=(), dma=False, extra=()):
        o = Op(eng, fn, dma)
        deps = []
        writes = [t.parent if isinstance(t, Sub) else t for t in writes] + [t.parent for t in reads if isinstance(t, Sub)]
        reads = [t for t in reads if not isinstance(t, Sub)]
        for t in reads:
            if t.w is not None:
                deps.append(t.w)
        for t in writes:
            if t.w is not None:
                deps.append(t.w)
            deps.extend(t.rc.values())
            deps.extend(t.rd)
        deps.extend(extra)
        if dma:
            half = self.nring // 2
            if eng == "pool":
                r = half + (self.npool % (self.nring - half))
                self.npool += 1
            else:
                r = self.nsp % half
                self.nsp += 1
            self.ndma += 1
            if self.ring_last[r] is not None:
                deps.append(self.ring_last[r])
            self.ring_cnt[r] += 1
            o.sem = r
            o.target = 16 * self.ring_cnt[r]
            self.ring_last[r] = o
            self.pending_dma.append(o)
        o.deps = [d for d in deps if d is not o]
        for t in reads:
            if dma:
                t.rd.append(o)
            else:
                t.rc[eng] = o
        for t in writes:
            t.w = o
            t.rc = {}
            t.rd = []
        o.idx = len(self.ops[eng])
        self.ops[eng].append(o)
        return o

    def barrier(self):
        lasts = [self.ops[e][-1] for e in ENGS if self.ops[e]]
        pend = list(self.pending_dma)
        for e in ENGS:
            self.op(e, lambda en: en.nop(), extra=lasts + pend)
        self.pending_dma = []
        for t in self.tiles:
            t.w = None
            t.rc = {}
            t.rd = []

    def finalize(self):
        for e in ENGS:
            seen = {a: -1 for a in ENGS}
            seen_d = {}
            for o in self.ops[e]:
                for d in o.deps:
                    if d.dma:
                        if seen_d.get(d.sem, 0) >= d.target:
                            continue
                        seen_d[d.sem] = d.target
                        o.waits.append(("d", d.sem, d.target))
                    else:
                        if d.eng == e and e == "pe":
                            continue
                        if seen[d.eng] >= d.idx:
                            continue
                        seen[d.eng] = d.idx
                        d.need_inc = True
                        o.waits.append(("c", d.eng, d))
        for e in ENGS:
            n = 0
            for o in self.ops[e]:
                if o.need_inc and not o.dma:
                    n += 1
                    o.tick = n

    def emit(self, nc, esems, dsems):
        handles = {"pe": "tensor", "act": "scalar", "dve": "vector", "pool": "gpsimd", "sp": "sync"}
        with nc.Block() as block:
            for e in ENGS:
                def body(en, e=e):
                    for o in self.ops[e]:
                        for w in o.waits:
                            if w[0] == "d":
                                en.wait_ge(dsems[w[1]], w[2])
                            else:
                                en.wait_ge(esems[w[1]], w[2].tick)
                        ins = o.fn(en)
                        if o.dma:
                            ins.then_inc(dsems[o.sem], 16)
                        elif o.need_inc:
                            ins.then_inc(esems[e], 1)
                getattr(block, handles[e])(body)


class Pool:
    def __init__(self, tiles):
        self.t = tiles
        self.i = 0

    def get(self):
        t = self.t[self.i % len(self.t)]
        self.i += 1
        return t


import os
DBG = set(os.environ.get("KDBG", "").split(","))


def build(stages, n_layers=DEPTH):
    nc = bass.Bass("TRN2", target_bir_lowering=False)
    xT = nc.dram_tensor("xT", [SPC, 128, 8, SEQ], F32, kind="ExternalInput").ap()
    memT = nc.dram_tensor("memT", [SPC, 128, 8, MEM], F32, kind="ExternalInput").ap()
    posb = nc.dram_tensor("posb", [SPC, 128, SEQ], I32, kind="ExternalInput").ap()
    wpack = nc.dram_tensor("wpack", [DEPTH, 128, LAY.tot], F32, kind="ExternalInput").ap()
    params = nc.dram_tensor("params", [128, NPAR], F32, kind="ExternalInput").ap()
    bcast = nc.dram_tensor("bcast", [DEPTH, 128, 1280], F32, kind="ExternalInput").ap()
    consts = nc.dram_tensor("consts", [128, 1664], F32, kind="ExternalInput").ap()
    outT = nc.dram_tensor("outT", [SPC, 128, 8, SEQ], F32, kind="ExternalOutput").ap()

    P = Prog()
    import contextlib
    es = contextlib.ExitStack()

    def sb(name, shape, dt):
        return es.enter_context(nc.sbuf_tensor(name, shape, dt))

    def ps(name, shape, dt):
        return es.enter_context(nc.psum_tensor(name, shape, dt))

    with es:
        XH = sb("XH", [128, 8, SEQ], BF16)
        XL = sb("XL", [128, 8, SEQ], BF16)
        BIGT = sb("BIGT", [128, 16384], F32)
        WP = sb("WP", [128, NSLOT, SLOT], BF16)
        PAR = sb("PAR", [128, NPAR], F32)
        BC = sb("BC", [128, 1280], F32)
        MASK = sb("MASK", [128, 2, 256], BF16)
        IDENT = sb("IDENT", [128, 128], BF16)
        IND = sb("IND", [128, 8, 128], BF16)
        ONESB = sb("ONESB", [128, 128], BF16)
        ONESF = sb("ONESF", [128, 128], F32)
        WST = sb("WST", [128, 4, 128], BF16)
        SGB = sb("SGB", [128, 512], BF16)
        LAM = sb("LAM", [128, 8], F32)
        TF = sb("TF", [128, 8, NB], F32)
        TB = sb("TB", [128, 8, NB], BF16)
        STG = sb("STG", [128, 1, 8, NB], F32)
        SCR = sb("SCR", [128, 8768], BF16)
        DED = sb("DED", [128, 4, NB], F32)
        KBAR = sb("KBAR", [128, 2, 8], BF16)
        KBF = sb("KBF", [128, 2, 8], F32)
        SMALL = sb("SMALL", [128, 64], F32)
        PSF = [ps(f"PSF{i}", [128, 2, NB], F32) for i in range(8)]
        PSH = PSF[7][:, :, :].bitcast(BF16)

        xh_t = [[P.tile(None, f"xh{b}_{m}") for m in range(8)] for b in range(NBLK)]
        wslots = Pool([P.tile(WP[:, i, :], f"w{i}") for i in range(NSLOT)])
        bank_t = [P.tile(None, f"bank{i}") for i in range(8)]
        PSA = Pool([Sub(PSF[bk][:, h, :], bank_t[bk]) for h in range(2) for bk in (0, 1, 2)])
        PSB = Pool([Sub(PSF[bk][:, h, :], bank_t[bk]) for h in range(2) for bk in (3, 4, 5, 6)])
        PSC = Pool([Sub(PSF[7][:, h, :], bank_t[7]) for h in range(2)])
        psh_t = bank_t[7]
        PSA2 = Pool([Sub(PSF[bk][:, :, :], bank_t[bk]) for bk in (0, 1, 2)])
        PSB2 = Pool([Sub(PSF[bk][:, :, :], bank_t[bk]) for bk in (3, 4, 5, 6)])
        tf = Pool([P.tile(TF[:, i, :], f"tf{i}") for i in range(8)])
        tb = Pool([P.tile(TB[:, i, :], f"tb{i}") for i in range(8)])
        tb2 = Pool([P.tile(TB[:, 2 * i:2 * i + 2, :], f"tbp{i}") for i in range(4)])
        stg = Pool([P.tile(STG[:, i], f"stg{i}") for i in range(1)])
        ded = [P.tile(DED[:, i, :], f"ded{i}") for i in range(4)]
        print("sbuf bytes remaining", nc.sbuf_bytes_remaining)
        par_t = P.tile(PAR, "par")
        bc_t = P.tile(BC, "bc")
        const_t = P.tile(None, "const")
        lam_t = P.tile(LAM, "lam")
        small_t = P.tile(SMALL, "small")
        acc_t = [[P.tile(None, f"acc{b}_{m}") for m in range(8)] for b in range(NBLK)]
        ka_t = [[P.tile(None, f"ka{b}_{c}") for c in range(4)] for b in range(NBLK)]
        kc_t = [[P.tile(None, f"kc{b}_{c}") for c in range(2)] for b in range(NBLK)]
        v_t = [[[P.tile(None, f"v{b}_{t}_{p}") for p in range(3)] for t in range(2)] for b in range(NBLK)]
        zz_t = [P.tile(None, f"zz{m}") for m in range(8)]
        zz2_t = [P.tile(None, f"zzb{m}") for m in range(8)]
        scr_t = {}

        def scr(name):
            if name not in scr_t:
                scr_t[name] = P.tile(None, "scr_" + name)
            return scr_t[name]

        BIGB = BIGT[:, :].bitcast(BF16)
        ACC = BIGT[:, :].rearrange("p (c t) -> p c t", c=8)
        KA = BIGB[:, 0:8192].rearrange("p (c t) -> p c t", c=4)
        KC = BIGB[:, 8192:12288].rearrange("p (c t) -> p c t", c=2)
        VV = BIGB[:, 12288:24576].rearrange("p (t c) -> p t c", t=16)
        ZZ = BIGT[:, 12288:14336].rearrange("p (c t) -> p c t", c=8)
        ZZ2 = BIGT[:, 14336:16384].rearrange("p (c t) -> p c t", c=8)
        QA = SCR[:, 0:2048].rearrange("p (c t) -> p c t", c=4)
        QC = SCR[:, 2048:3072].rearrange("p (c t) -> p c t", c=2)
        UU = SCR[:, 3072:4096].rearrange("p (c t) -> p c t", c=4)
        VS = SCR[:, 4096:4608].rearrange("p (t c) -> p t c", t=2)
        MIX = SCR[:, 4608:7168].rearrange("p (c t) -> p c t", c=10)
        BIAST = SCR[:, 7680:8704].rearrange("p (h t) -> p h t", h=4)
        BIASQ = SCR[:, 8704:8768].rearrange("p (t h n) -> p t h n", t=2, h=4)
        QX = SCR[:, 0:2048].rearrange("p (c t) -> p c t", c=8)
        XO = SCR[:, 2048:4096].rearrange("p (c t) -> p c t", c=8)
        MEMB = BIGB[:, 0:2048].rearrange("p (c t) -> p c t", c=8)
        KM = BIGB[:, 2048:4096].rearrange("p (c t) -> p c t", c=8)
        VM = BIGB[:, 4096:6144].rearrange("p (t c) -> p t c", t=2)

        def dma_in(eng, out_ap, in_ap, wt, rt=()):
            return P.op(eng, lambda e: e.dma_start(out=out_ap, in_=in_ap), reads=rt, writes=wt, dma=True)

        def wload(l, name):
            o, s = LAY.off[name]
            t = wslots.get()
            dma_in("pool", t.ap[:, 0:s], wpack[l, :, o:o + s], [t])
            return t

        def mm(out_t, out_ap, lt, l_ap, rt, r_ap, start, stop):
            rd = [x for x in (lt, rt) if x is not None]
            if not start:
                rd.append(out_t)
            P.op("pe", lambda e: e.matmul(out_ap, l_ap, r_ap, start=start, stop=stop), reads=rd, writes=[out_t])

        def act(out_t, out_ap, in_ts, in_ap, func, bias=0.0, scale=1.0, extra_w=()):
            P.op("act", lambda e: e.activation(out_ap, in_ap, func, bias=bias, scale=scale),
                 reads=in_ts, writes=[out_t] + list(extra_w))

        def tt(eng, out_t, out_ap, rts, a_ap, b_ap, op):
            P.op(eng, lambda e: e.tensor_tensor(out_ap, a_ap, b_ap, op), reads=rts, writes=[out_t])

        def stt(eng, out_t, out_ap, rts, a_ap, scalar, b_ap, op0, op1):
            P.op(eng, lambda e: e.scalar_tensor_tensor(out_ap, a_ap, scalar, b_ap, op0, op1), reads=rts, writes=[out_t])

        def ts(eng, out_t, out_ap, rts, a_ap, s1, s2, op0, op1=None):
            if op1 is None:
                P.op(eng, lambda e: e.tensor_scalar(out_ap, a_ap, s1, None, op0), reads=rts, writes=[out_t])
            else:
                P.op(eng, lambda e: e.tensor_scalar(out_ap, a_ap, s1, s2, op0, op1), reads=rts, writes=[out_t])

        def rsqrt(out_t, out_ap, rts, in_ap, mul, add):
            ts("dve", out_t, out_ap, rts, in_ap, mul, add, ALU.mult, ALU.add)
            P.op("act", lambda e: e.activation(out_ap, out_ap, AF.Ln), reads=[out_t], writes=[out_t])
            P.op("act", lambda e: e.activation(out_ap, out_ap, AF.Exp, scale=-0.5), reads=[out_t], writes=[out_t])

        def recip(out_t, out_ap, in_t, in_ap):
            P.op("act", lambda e: e.activation(out_ap, in_ap, AF.Ln), reads=[in_t], writes=[out_t])
            P.op("act", lambda e: e.activation(out_ap, out_ap, AF.Exp, scale=-1.0), reads=[out_t], writes=[out_t])

        def cp(eng, out_t, out_ap, rts, in_ap):
            if eng == "act":
                P.op("act", lambda e: e.copy(out_ap, in_ap), reads=rts, writes=[out_t])
            else:
                P.op(eng, lambda e: e.tensor_copy(out_ap, in_ap), reads=rts, writes=[out_t])

        def bs(b):
            return slice(b * NB, (b + 1) * NB)

        def pcol(key, m=0):
            c = PCOLS[key] + m
            return PAR[:, c:c + 1]

        dma_in("sp", PAR[:, :], params[:, :], [par_t])
        dma_in("pool", MASK[:, :, :], consts[:, 0:512].rearrange("p (j t) -> p j t", j=2), [const_t])
        dma_in("pool", IDENT[:, :], consts[:, 512:640], [const_t])
        dma_in("pool", IND[0:8, :, :], consts[0:8, 640:1664].rearrange("p (b t) -> p b t", b=8), [const_t])
        P.op("dve", lambda e: e.memset(ONESB[:, :], 1.0), writes=[const_t])
        P.op("dve", lambda e: e.memset(ONESF[:, :], 1.0), writes=[const_t])

        def split_hilo(b, m, r_t, r_ap):
            cp("act", xh_t[b][m], XH[:, m, bs(b)], [r_t], r_ap)
            tt("dve", xh_t[b][m], XL[:, m, bs(b)], [r_t, xh_t[b][m]], r_ap, XH[:, m, bs(b)], ALU.subtract)

        def load_x(s):
            for b in range(NBLK):
                dma_in("sp", ACC[:, :, bs(b)], xT[s, :, :, bs(b)], acc_t[b])
            for b in range(NBLK):
                for m in range(8):
                    cp("act", xh_t[b][m], XH[:, m, bs(b)], [acc_t[b][m]], ACC[:, m, bs(b)])
                for m in range(8):
                    tt("dve", xh_t[b][m], XL[:, m, bs(b)], [acc_t[b][m], xh_t[b][m]], ACC[:, m, bs(b)], XH[:, m, bs(b)], ALU.subtract)

        def store_x(s):
            for b in range(NBLK):
                for m in range(8):
                    tt("dve", acc_t[b][m], ACC[:, m, bs(b)], [xh_t[b][m]], XH[:, m, bs(b)], XL[:, m, bs(b)], ALU.add)
                P.op("sp", lambda e, b=b: e.dma_start(out=outT[s, :, :, bs(b)], in_=ACC[:, :, bs(b)]), reads=list(acc_t[b]), dma=True)

        ln_sums = {}

        def ln_A(l, which, b, ysrc, zt, z_ap, dset=0):
            ln_A1(l, which, b, ysrc, zt, z_ap, dset)
            ln_A2(l, which, b, zt, z_ap, dset)

        def ln_A1(l, which, b, ysrc, zt, z_ap, dset=0):
            s1 = PSC.get()
            s2 = PSC.get()
            ln_sums[dset] = (s1, s2)
            for m in range(8):
                yt, yap = ysrc(m)
                stt("dve", zt[m], z_ap[:, m, :], [xh_t[b][m], yt], XH[:, m, bs(b)], ALPHA, yap, ALU.mult, ALU.add)
            for m in range(8):
                stt("dve", zt[m], z_ap[:, m, :], [xh_t[b][m], zt[m]], XL[:, m, bs(b)], ALPHA, z_ap[:, m, :], ALU.mult, ALU.add)
                mm(s1, s1.ap, const_t, ONESF[:, :], zt[m], z_ap[:, m, :], m == 0, m == 7)
            for m in range(8):
                q = tf.get()
                act(q, q.ap, [zt[m]], z_ap[:, m, :], AF.Square)
                mm(s2, s2.ap, const_t, ONESF[:, :], q, q.ap, m == 0, m == 7)

        def ln_A2(l, which, b, zt, z_ap, dset=0):
            s1, s2 = ln_sums[dset]
            mean = ded[2 * dset]
            ts("dve", mean, mean.ap, [s1], s1.ap, 1.0 / D, None, ALU.mult)
            msq = tf.get()
            tt("dve", msq, msq.ap, [mean], mean.ap, mean.ap, ALU.mult)
            var = tf.get()
            stt("dve", var, var.ap, [s2, msq], s2.ap, 1.0 / D, msq.ap, ALU.mult, ALU.subtract)
            rstd = ded[2 * dset + 1]
            rsqrt(rstd, rstd.ap, [var], var.ap, 1.0, EPS)
            tt("dve", mean, mean.ap, [mean, rstd], mean.ap, rstd.ap, ALU.mult)

        def ln_B(l, which, b, zt, z_ap, dset=0):
            mean = ded[2 * dset]
            rstd = ded[2 * dset + 1]
            for m in range(8):
                tt("dve", zt[m], z_ap[:, m, :], [zt[m], rstd], z_ap[:, m, :], rstd.ap, ALU.mult)
            for m in range(8):
                tt("dve", zt[m], z_ap[:, m, :], [zt[m], mean], z_ap[:, m, :], mean.ap, ALU.subtract)
            for m in range(8):
                act(xh_t[b][m], XH[:, m, bs(b)], [zt[m], par_t], z_ap[:, m, :], AF.Identity,
                    bias=pcol((l, which + "_b"), m), scale=pcol((l, which + "_g"), m))
            for m in range(8):
                act(zt[m], z_ap[:, m, :], [zt[m], par_t], z_ap[:, m, :], AF.Identity,
                    bias=pcol((l, which + "_b"), m), scale=pcol((l, which + "_g"), m))
            for m in range(8):
                tt("dve", xh_t[b][m], XL[:, m, bs(b)], [zt[m], xh_t[b][m]], z_ap[:, m, :], XH[:, m, bs(b)], ALU.subtract)

        def ln_block(l, which, b, ysrc, zt, z_ap):
            ln_A(l, which, b, ysrc, zt, z_ap, 0)
            ln_B(l, which, b, zt, z_ap, 0)

        def ffn(l, f, which):
            rem = NJ % JG
            groups = ([list(range(0, rem))] if rem else []) + [list(range(j, j + JG)) for j in range(rem, NJ, JG)]

            def lnA1(b):
                ln_A1(l, which, b, lambda m, b=b: (acc_t[b][m], ACC[:, m, bs(b)]), acc_t[b], ACC[:, :, bs(b)], b % 2)

            def lnA2(b):
                ln_A2(l, which, b, acc_t[b], ACC[:, :, bs(b)], b % 2)

            def lnB(b):
                ln_B(l, which, b, acc_t[b], ACC[:, :, bs(b)], b % 2)
            if "g1" in DBG:
                groups = groups[:1]
            for gi, grp in enumerate(groups):
                wg = {j: wload(l, f"g{f}_{j}") for j in grp}
                wu = {j: wload(l, f"u{f}_{j}") for j in grp}
                wd = {j: wload(l, f"d{f}_{j}") for j in grp}
                for b in range(NBLK if "nomm" not in DBG else 0):
                    hs = {}
                    for j in grp:
                        pg = PSA.get()
                        pu = PSA.get()
                        for k in range(8):
                            mm(pg, pg.ap, wg[j], wg[j].ap[:, k * 128:(k + 1) * 128], xh_t[b][k], XH[:, k, bs(b)], k == 0, k == 7)
                        for k in range(8):
                            mm(pu, pu.ap, wu[j], wu[j].ap[:, k * 128:(k + 1) * 128], xh_t[b][k], XH[:, k, bs(b)], k == 0, k == 7)
                        if "nocons" in DBG:
                            continue
                        sg = tf.get()
                        if "nosilu" in DBG:
                            cp("act", sg, sg.ap, [pg], pg.ap)
                        else:
                            act(sg, sg.ap, [pg], pg.ap, AF.Silu)
                        h = tb.get()
                        if "nott" in DBG:
                            cp("dve", h, h.ap, [sg], sg.ap)
                            cp("dve", h, h.ap, [pu], pu.ap)
                        else:
                            tt("dve", h, h.ap, [sg, pu], sg.ap, pu.ap, ALU.mult)
                        hs[j] = h
                    pd = [PSB.get() for _ in range(8)]
                    for m in range(8 if "nodown" not in DBG else 0):
                        for ji, j in enumerate(grp):
                            mm(pd[m], pd[m].ap, wd[j], wd[j].ap[:, m * 128:(m + 1) * 128], hs[j], hs[j].ap, ji == 0, ji == len(grp) - 1)
                    for m in range(8):
                        if gi == 0:
                            act(acc_t[b][m], ACC[:, m, bs(b)], [pd[m]], pd[m].ap, AF.Copy, scale=0.5)
                        else:
                            stt("dve", acc_t[b][m], ACC[:, m, bs(b)], [pd[m], acc_t[b][m]], pd[m].ap, 0.5, ACC[:, m, bs(b)], ALU.mult, ALU.add)
                    if gi == len(groups) - 1:
                        if b >= 1:
                            lnA1(b - 1)
                        if b >= 2:
                            lnB(b - 2)
                        if b >= 1:
                            lnA2(b - 1)
            lnA1(NBLK - 1)
            lnB(NBLK - 2)
            lnA2(NBLK - 1)
            lnB(NBLK - 1)

        def rope_tables(s, b, dset=1):
            pi_t = tf.get()
            P.op("sp", lambda e: e.dma_start(out=pi_t.ap.bitcast(I32), in_=posb[s, :, bs(b)]), writes=[pi_t], dma=True)
            ang = tf.get()
            cp("dve", ang, ang.ap, [pi_t], pi_t.ap.bitcast(I32))
            ts("dve", ang, ang.ap, [ang, par_t], ang.ap, pcol("invf"), None, ALU.mult)
            C1 = 6.28125
            C2 = 2.0 * math.pi - C1

            def reduce(add):
                t = tf.get()
                ki = tf.get()
                if add:
                    ts("dve", t, t.ap, [ang], ang.ap, 1.0 / (2.0 * math.pi), add / (2.0 * math.pi), ALU.mult, ALU.add)
                else:
                    ts("dve", t, t.ap, [ang], ang.ap, 1.0 / (2.0 * math.pi), None, ALU.mult)
                cp("dve", ki, ki.ap.bitcast(I32), [t], t.ap)
                cp("dve", t, t.ap, [ki], ki.ap.bitcast(I32))
                if add:
                    stt("dve", ki, ki.ap, [t, ang], t.ap, -C1, ang.ap, ALU.mult, ALU.add)
                    ts("dve", ki, ki.ap, [ki], ki.ap, add, None, ALU.add)
                else:
                    stt("dve", ki, ki.ap, [t, ang], t.ap, -C1, ang.ap, ALU.mult, ALU.add)
                stt("dve", ki, ki.ap, [t, ki], t.ap, -C2, ki.ap, ALU.mult, ALU.add)
                ts("dve", ki, ki.ap, [ki], ki.ap, -math.pi, None, ALU.max)
                ts("dve", ki, ki.ap, [ki], ki.ap, math.pi, None, ALU.min)
                return ki
            sn = ded[2 * dset + 1]
            cs = ded[2 * dset]
            r1 = reduce(0.0)
            act(sn, sn.ap, [r1], r1.ap, AF.Sin)
            r2 = reduce(0.5 * math.pi)
            act(cs, cs.ap, [r2], r2.ap, AF.Sin)
            ts("dve", sn, sn.ap, [sn, par_t], sn.ap, pcol("sign"), None, ALU.mult)
            ts("dve", sn, sn.ap, [sn], sn.ap, -1.0, None, ALU.mult)
            return cs, sn

        def proj_rope(l, wa, was, b, cs, sn, out_t, out_ap, bd=False):
            p1 = PSA.get()
            p2 = PSA.get()
            for k in range(8):
                mm(p1, p1.ap, wa, wa.ap[:, k * 128:(k + 1) * 128], xh_t[b][k], XH[:, k, bs(b)], k == 0, k == 7)
            for k in range(8):
                mm(p2, p2.ap, was, was.ap[:, k * 128:(k + 1) * 128], xh_t[b][k], XH[:, k, bs(b)], k == 0, k == 7)
            t1 = tf.get()
            tt("dve", t1, t1.ap, [p1, cs], p1.ap, cs.ap, ALU.mult)
            t2 = tf.get()
            tt("dve", t2, t2.ap, [p2, sn], p2.ap, sn.ap, ALU.mult)
            if bd:
                tt("dve", out_t, out_ap[0:64, 0:NB], [t1, t2], t1.ap[0:64, :], t2.ap[0:64, :], ALU.add)
                tt("dve", out_t, out_ap[64:128, NB:2 * NB], [t1, t2], t1.ap[64:128, :], t2.ap[64:128, :], ALU.add)
            else:
                tt("dve", out_t, out_ap, [t1, t2], t1.ap, t2.ap, ALU.add)

        def mixer_consts(l):
            P.op("dve", lambda e: e.memset(SCR[:, 0:3072], 0.0), writes=[scr("qa"), scr("qc")])
            dma_in("sp", BC[:, :], bcast[l, :, :], [bc_t])
            o, s_ = LAY.off["wst"]
            wt = wslots.get()
            dma_in("pool", wt.ap[:, 0:512], wpack[l, :, o:o + 512], [wt])
            for g in range(4):
                tt("dve", const_t, WST[:, g, :], [wt, const_t], wt.ap[:, g * 128:(g + 1) * 128], MASK[:, 0, 0:128], ALU.mult)
            cp("dve", const_t, SGB[:, :], [bc_t], BC[:, 768:1280])
            lam_init = 0.8 - 0.6 * math.exp(-0.3 * l)
            t = tf.get()
            tt("dve", t, t.ap[:, 0:64], [bc_t], BC[:, 512:576], BC[:, 576:640], ALU.mult)
            tt("dve", t, t.ap[:, 64:128], [bc_t], BC[:, 640:704], BC[:, 704:768], ALU.mult)
            P.op("dve", lambda e: e.reduce_sum(LAM[:, 0:1], t.ap[:, 0:64], AX.X), reads=[t], writes=[lam_t])
            P.op("dve", lambda e: e.reduce_sum(LAM[:, 1:2], t.ap[:, 64:128], AX.X), reads=[t], writes=[lam_t])
            act(lam_t, LAM[:, 2:4], [lam_t], LAM[:, 0:2], AF.Exp)
            tt("dve", lam_t, LAM[:, 4:5], [lam_t], LAM[:, 3:4], LAM[:, 2:3], ALU.subtract)
            ts("dve", lam_t, LAM[:, 4:5], [lam_t], LAM[:, 4:5], -lam_init, None, ALU.add)
            ts("dve", lam_t, LAM[:, 5:6], [par_t], pcol((l, "subln")), 1.0 - lam_init, None, ALU.mult)

        def mixer_phase1(l, s):
            wv = [wload(l, f"wv_{k}") for k in range(8)]
            for b in range(NBLK):
                for tI in range(2):
                    tok = slice(b * NB + tI * 128, b * NB + (tI + 1) * 128)
                    pa = PSA.get()
                    pb = PSA.get()
                    for k in range(8):
                        mm(pa, pa.ap, xh_t[b][k], XH[:, k, tok], wv[k], wv[k].ap[:, 0:256], k == 0, k == 7)
                    for k in range(8):
                        mm(pb, pb.ap, xh_t[b][k], XH[:, k, tok], wv[k], wv[k].ap[:, 256:512], k == 0, k == 7)
                    cp("act", v_t[b][tI][0], VV[:, 2 * b + tI, 0:256], [pa], pa.ap)
                    cp("act", v_t[b][tI][1], VV[:, 2 * b + tI, 256:512], [pb], pb.ap)
                    pc = PSA.get()
                    for k in range(8):
                        mm(pc, pc.ap, xh_t[b][k], XH[:, k, tok], wv[k], wv[k].ap[:, 512:768], k == 0, k == 7)
                    cp("act", v_t[b][tI][2], VV[:, 2 * b + tI, 512:768], [pc], pc.ap)
            kw = []
            for c in range(6):
                if c < 4:
                    kw.append((wload(l, f"ka_{c}"), wload(l, f"kas_{c}")))
                else:
                    kw.append((wload(l, f"kc_{c - 4}"), wload(l, f"kcs_{c - 4}")))
            tabs = {0: rope_tables(s, 0, 0)}
            for b in range(NBLK):
                if b + 1 < NBLK:
                    tabs[b + 1] = rope_tables(s, b + 1, (b + 1) % 2)
                cs, sn = tabs[b]
                for c in range(6):
                    dst = KA[:, c, bs(b)] if c < 4 else KC[:, c - 4, bs(b)]
                    proj_rope(l, kw[c][0], kw[c][1], b, cs, sn, ka_t[b][c] if c < 4 else kc_t[b][c - 4], dst)
            for c in range(2):
                for b in range(NBLK):
                    P.op("dve", lambda e, c=c, b=b: e.reduce_sum(KBF[:, c, b:b + 1], KC[:, c, bs(b)], AX.X),
                         reads=[kc_t[b][c]], writes=[small_t])
            ts("dve", small_t, KBAR[:, :, :], [small_t], KBF[:, :, :], 1.0 / 256.0, None, ALU.mult)

        def run_streams(streams, nkt):
            pend = [sc(0) for sc, _ in streams]
            for kt in range(nkt):
                for i, (sc, fin) in enumerate(streams):
                    nx = sc(kt + 1) if kt + 1 < nkt else None
                    fin(kt, pend[i])
                    pend[i] = nx

        def mixer_phase2(l, s):
            lam_init = 0.8 - 0.6 * math.exp(-0.3 * l)
            def front(b):
                cs, sn = rope_tables(s, b)
                for c in range(4):
                    wa, was = wload(l, f"qa_{c}"), wload(l, f"qas_{c}")
                    proj_rope(l, wa, was, b, cs, sn, scr("qa"), QA[:, c, :], bd=True)
                for c in range(2):
                    wa, was = wload(l, f"qc_{c}"), wload(l, f"qcs_{c}")
                    proj_rope(l, wa, was, b, cs, sn, scr("qc"), QC[:, c, :], bd=True)
                for g in range(4 if "nosgu" not in DBG else 0):
                    w = wload(l, f"u_{g}")
                    p = PSA.get()
                    for k in range(8):
                        mm(p, p.ap[0:64, :], w, w.ap[:, k * 64:(k + 1) * 64], xh_t[b][k], XH[:, k, bs(b)], k == 0, k == 7)
                    act(scr("u"), UU[0:64, g, :], [p], p.ap[0:64, :], AF.Gelu)
                wvs = [wload(l, f"vs_{h}") for h in range(2)]
                for tI in range(2 if "nosgu" not in DBG else 0):
                    tok = slice(b * NB + tI * 128, b * NB + (tI + 1) * 128)
                    p = PSA.get()
                    for k in range(8):
                        w = wvs[k // 4]
                        mm(p, p.ap, xh_t[b][k], XH[:, k, tok], w, w.ap[:, (k % 4) * 256:(k % 4 + 1) * 256], k == 0, k == 7)
                    gv = tf.get()
                    act(gv, gv.ap, [p], p.ap, AF.Gelu)
                    sq = tf.get()
                    P.op("dve", lambda e, gv=gv: e.reduce_sum(SMALL[:, 0:1], gv.ap, AX.X), reads=[gv], writes=[small_t])
                    act(sq, sq.ap, [gv], gv.ap, AF.Square)
                    P.op("dve", lambda e, sq=sq: e.reduce_sum(SMALL[:, 1:2], sq.ap, AX.X), reads=[sq], writes=[small_t])
                    ts("dve", small_t, SMALL[:, 2:3], [small_t], SMALL[:, 0:1], 1.0 / 256.0, None, ALU.mult)
                    tt("dve", small_t, SMALL[:, 3:4], [small_t], SMALL[:, 2:3], SMALL[:, 2:3], ALU.mult)
                    stt("dve", small_t, SMALL[:, 4:5], [small_t], SMALL[:, 1:2], 1.0 / 256.0, SMALL[:, 3:4], ALU.mult, ALU.subtract)
                    rsqrt(small_t, SMALL[:, 5:6], [small_t], SMALL[:, 4:5], 1.0, EPS)
                    ts("dve", gv, gv.ap, [gv, small_t], gv.ap, SMALL[:, 2:3], None, ALU.subtract)
                    ts("dve", gv, gv.ap, [gv, small_t], gv.ap, SMALL[:, 5:6], None, ALU.mult)
                    tt("dve", gv, gv.ap, [gv, bc_t], gv.ap, BC[:, 0:256], ALU.mult)
                    tt("dve", scr("vs"), VS[:, tI, :], [gv, bc_t], gv.ap, BC[:, 256:512], ALU.add)
                for tI in range(2 if "nogate" not in DBG else 0):
                    qblk = b
                    pg = PSA.get()
                    for h in range(4):
                        rows = slice((h % 2) * 64, (h % 2) * 64 + 64)
                        mm(pg, pg.ap[:, h * 8:(h + 1) * 8], scr("qc"), QC[rows, h // 2, (h % 2) * NB + tI * 128:(h % 2) * NB + (tI + 1) * 128],
                           small_t, KBAR[rows, h // 2, :], True, True)
                    gt = tf.get()
                    cp("dve", gt, gt.ap[:, 0:32], [pg], pg.ap[:, 0:32])
                    g3 = gt.ap[:, 0:32].rearrange("p (h n) -> p h n", h=4)
                    if qblk < 8:
                        P.op("dve", lambda e, g3=g3, qblk=qblk: e.memset(g3[:, :, qblk:8], -1e30), reads=[gt], writes=[gt])
                    for h in range(4):
                        P.op("dve", lambda e, gt=gt, h=h: e.max(gt.ap[:, 64 + h * 8:72 + h * 8], gt.ap[:, h * 8:(h + 1) * 8]),
                             reads=[gt], writes=[gt])
                    for h in range(4):
                        ts("dve", gt, gt.ap[:, 128 + h * 8:136 + h * 8], [gt], gt.ap[:, h * 8:(h + 1) * 8],
                           gt.ap[:, 64 + h * 8 + 2:64 + h * 8 + 3], None, ALU.is_ge)
                    ts("dve", gt, gt.ap[:, 160:192], [gt], gt.ap[:, 128:160], BIG, -BIG, ALU.mult, ALU.add)
                    b3 = gt.ap[:, 160:192].rearrange("p (h n) -> p h n", h=4)
                    P.op("dve", lambda e, b3=b3, qblk=qblk: e.memset(b3[:, :, qblk:8], 0.0), reads=[gt], writes=[gt])
                    cp("dve", scr("biasq"), BIASQ[:, tI, :, :], [gt], b3)
                for h in range(4 if "nogate" not in DBG else 0):
                    p = PSA.get()
                    for tI in range(2):
                        mm(p, p.ap[0:8, tI * 128:(tI + 1) * 128], scr("biasq"), BIASQ[:, tI, h, :], const_t, IDENT[:, :], True, True)
                    cp("dve", scr("biast"), BIAST[0:8, h, :], [p], p.ap[0:8, :])

            def mid(b):
                nkt = 2 * b + 2
                for g in range(4 if "nosgu" not in DBG else 0):
                    p = PSA.get()
                    for tI in range(2):
                        cols = slice(tI * 128, (tI + 1) * 128)
                        mm(p, p.ap[0:64, cols], scr("vs"), VS[:, tI, g * 64:(g + 1) * 64], const_t, WST[:, g, :], True, True)
                    sm = tf.get()
                    for tI in range(2):
                        cols = slice(tI * 128, (tI + 1) * 128)
                        tt("dve", sm, sm.ap[0:64, cols], [p, bc_t], p.ap[0:64, cols], BC[0:64, 768 + g * 128:768 + (g + 1) * 128], ALU.add)
                    tt("dve", scr("mix"), MIX[0:64, 4 + g, :], [scr("u"), sm], UU[0:64, g, :], sm.ap[0:64, :], ALU.mult)
                for h0 in range(0, 4 if "noA" not in DBG else 0, 2):
                    streams = []
                    accs = []
                    for h in (h0, h0 + 1):
                        obank = PSB2.get()
                        sbank = PSB2.get()
                        accs.append((h, obank, sbank))

                        def a_score(kt, h=h):
                            bank = PSA2.get()
                            mm(bank, bank.ap.rearrange("p a t -> p (a t)"), ka_t[kt // 2][h], KA[:, h, kt * 128:(kt + 1) * 128],
                               scr("qa"), QA[:, h, :], True, True)
                            return bank

                        def a_finish(kt, bank, h=h, obank=obank, sbank=sbank, b=b, nkt=nkt):
                            vt = v_t[kt // 2][kt % 2][h // 2]
                            pt = tb2.get()
                            act(pt, pt.ap, [bank], bank.ap, AF.Exp, scale=0.125)
                            if kt >= 2 * b:
                                for mp in range(2):
                                    tt("dve", pt, pt.ap[:, mp, :], [pt, const_t], pt.ap[:, mp, :], MASK[:, kt - 2 * b, :], ALU.mult)
                            p2d = pt.ap.rearrange("p a t -> p (a t)")
                            mm(obank, obank.ap.rearrange("p a t -> p (a t)"), vt, VV[:, kt, h * 128:(h + 1) * 128], pt, p2d, kt == 0, kt == nkt - 1)
                            mm(sbank, sbank.ap.rearrange("p a t -> p (a t)"), const_t, ONESB[:, :], pt, p2d, kt == 0, kt == nkt - 1)
                        streams.append((a_score, a_finish))
                    run_streams(streams, nkt)
                    for (h, obank, sbank) in accs:
                        po = [Sub(obank.ap[:, 0, :], obank.parent), Sub(sbank.ap[:, 0, :], sbank.parent),
                              Sub(obank.ap[:, 1, :], obank.parent), Sub(sbank.ap[:, 1, :], sbank.parent)]
                        r1 = tf.get()
                        recip(r1, r1.ap, po[1], po[1].ap)
                        r2 = tf.get()
                        recip(r2, r2.ap, po[3], po[3].ap)
                        tt("dve", r1, r1.ap, [r1, po[0]], r1.ap, po[0].ap, ALU.mult)
                        tt("dve", r2, r2.ap, [r2, po[2]], r2.ap, po[2].ap, ALU.mult)
                        o = tf.get()
                        stt("dve", o, o.ap, [r2, lam_t, r1], r2.ap, LAM[:, 4:5], r1.ap, ALU.mult, ALU.add)
                        sq = tf.get()
                        act(sq, sq.ap, [o], o.ap, AF.Square)
                        ss = PSC.get()
                        mm(ss, ss.ap, const_t, ONESF[:, :], sq, sq.ap, True, True)
                        rr = tf.get()
                        rsqrt(rr, rr.ap, [ss], ss.ap, 1.0 / 128.0, EPS)
                        tt("dve", o, o.ap, [o, rr], o.ap, rr.ap, ALU.mult)
                        ts("dve", scr("mix"), MIX[:, h, :], [o, lam_t], o.ap, LAM[:, 5:6], None, ALU.mult)
                streams = []
                accs = []
                for hp in range(2 if "noC" not in DBG else 0):
                    obank = PSB2.get()
                    sbank = PSB2.get()
                    accs.append((hp, obank, sbank))

                    def c_score(kt, hp=hp):
                        bank = PSA2.get()
                        o2d = bank.ap.rearrange("p a t -> p (a t)")
                        mm(bank, o2d, kc_t[kt // 2][hp], KC[:, hp, kt * 128:(kt + 1) * 128], scr("qc"), QC[:, hp, :], True, False)
                        mm(bank, o2d, const_t, IND[0:8, kt // 2, :], scr("biast"),
                           BIAST[0:8, 2 * hp:2 * hp + 2, :].rearrange("p h t -> p (h t)"), False, True)
                        return bank

                    def c_finish(kt, bank, hp=hp, obank=obank, sbank=sbank, b=b, nkt=nkt):
                        pt = tb2.get()
                        act(pt, pt.ap, [bank], bank.ap, AF.Exp, scale=0.125)
                        if kt >= 2 * b:
                            for hh in range(2):
                                tt("dve", pt, pt.ap[:, hh, :], [pt, const_t], pt.ap[:, hh, :], MASK[:, kt - 2 * b, :], ALU.mult)
                        vt = v_t[kt // 2][kt % 2][2]
                        p2d = pt.ap.rearrange("p a t -> p (a t)")
                        mm(obank, obank.ap.rearrange("p a t -> p (a t)"), vt, VV[:, kt, 512 + hp * 128:512 + (hp + 1) * 128], pt, p2d, kt == 0, kt == nkt - 1)
                        mm(sbank, sbank.ap.rearrange("p a t -> p (a t)"), const_t, ONESB[:, :], pt, p2d, kt == 0, kt == nkt - 1)
                    streams.append((c_score, c_finish))
                if streams:
                    run_streams(streams, nkt)
                for (hp, obank, sbank) in accs:
                    for hh in range(2):
                        rws = slice(hh * 64, hh * 64 + 64)
                        r1 = tf.get()
                        recip(r1, r1.ap[rws, :], sbank, sbank.ap[rws, hh, :])
                        tt("dve", scr("mix"), MIX[rws, 8 + hp, :], [r1, obank], r1.ap[rws, :], obank.ap[rws, hh, :], ALU.mult)

            def tail(b):
                wo = [wload(l, f"wo_{c}") for c in range(10)]
                ys = {}

                def ysrc(m, b=b, wo=wo):
                    p = PSA.get()
                    for c in range(10):
                        kk = 64 if 4 <= c < 8 else 128
                        mm(p, p.ap, wo[c], wo[c].ap[0:kk, m * 128:(m + 1) * 128], scr("mix"), MIX[0:kk, c, :], c == 0, c == 9)
                    return p, p.ap
                zt_, za_ = (zz_t, ZZ) if b % 2 == 0 else (zz2_t, ZZ2)
                ln_A1(l, "ln2", b, ysrc, zt_, za_, 0)
                if b >= 1:
                    pz, pa = (zz_t, ZZ) if (b - 1) % 2 == 0 else (zz2_t, ZZ2)
                    ln_B(l, "ln2", b - 1, pz, pa, 0)
                ln_A2(l, "ln2", b, zt_, za_, 0)


            front(0)
            mid(0)
            for b in range(NBLK):
                if b + 1 < NBLK:
                    front(b + 1)
                tail(b)
                if b + 1 < NBLK:
                    mid(b + 1)
            ln_B(l, "ln2", NBLK - 1, zz2_t, ZZ2, 0)

        def xattn(l, s):
            st = stg.get()
            dma_in("sp", st.ap[:, :, 0:MEM], memT[s, :, :, :], [st])
            cp("dve", scr("memb"), MEMB[:, :, :], [st], st.ap[:, :, 0:MEM])
            for m in range(8):
                w = wload(l, f"xk_{m}")
                p = PSA.get()
                for k in range(8):
                    mm(p, p.ap, w, w.ap[:, k * 128:(k + 1) * 128], scr("memb"), MEMB[:, k, :], k == 0, k == 7)
                cp("act", scr("km"), KM[:, m, :], [p], p.ap)
            wv = [wload(l, f"xv_{k}") for k in range(8)]
            for mt in range(2):
                for n in range(4):
                    p = PSA.get()
                    for k in range(8):
                        mm(p, p.ap, scr("memb"), MEMB[:, k, mt * 128:(mt + 1) * 128], wv[k], wv[k].ap[:, n * 256:(n + 1) * 256], k == 0, k == 7)
                    cp("act", scr("vm"), VM[:, mt, n * 256:(n + 1) * 256], [p], p.ap)
            def xq(b):
                for m in range(8):
                    w = wload(l, f"xq_{m}")
                    p = PSA.get()
                    for k in range(8):
                        mm(p, p.ap, w, w.ap[:, k * 128:(k + 1) * 128], xh_t[b][k], XH[:, k, bs(b)], k == 0, k == 7)
                    cp("act", scr("qx"), QX[:, m, :], [p], p.ap)

            def xatt(b):
                def x_score(h):
                    bank = PSA2.get()
                    for mt in range(2):
                        for c in range(2):
                            mm(bank, bank.ap[:, mt, :], scr("km"), KM[:, 2 * h + c, mt * 128:(mt + 1) * 128], scr("qx"), QX[:, 2 * h + c, :], c == 0, c == 1)
                    return bank

                def x_finish(h, bank):
                    pt = tb2.get()
                    act(pt, pt.ap, [bank], bank.ap, AF.Exp, scale=1.0 / 16.0)
                    pS = PSB.get()
                    for mt in range(2):
                        mm(pS, pS.ap, const_t, ONESB[:, :], pt, pt.ap[:, mt, :], mt == 0, mt == 1)
                    r1 = tf.get()
                    recip(r1, r1.ap, pS, pS.ap)
                    for c in range(2):
                        po = PSB.get()
                        for mt in range(2):
                            mm(po, po.ap, scr("vm"), VM[:, mt, (2 * h + c) * 128:(2 * h + c + 1) * 128], pt, pt.ap[:, mt, :], mt == 0, mt == 1)
                        tt("dve", scr("xo"), XO[:, 2 * h + c, :], [r1, po], r1.ap, po.ap, ALU.mult)
                pend = x_score(0)
                for h in range(4):
                    nxt = x_score(h + 1) if h < 3 else None
                    x_finish(h, pend)
                    pend = nxt

            def xtail(b):
                wo = [wload(l, f"xo_{c}") for c in range(8)]

                def ysrc(m, wo=wo):
                    p = PSA.get()
                    for c in range(8):
                        mm(p, p.ap, wo[c], wo[c].ap[:, m * 128:(m + 1) * 128], scr("xo"), XO[:, c, :], c == 0, c == 7)
                    return p, p.ap
                zt_, za_ = (zz_t, ZZ) if b % 2 == 0 else (zz2_t, ZZ2)
                ln_A1(l, "ln3", b, ysrc, zt_, za_, b % 2)
                if b >= 1:
                    pz, pa = (zz_t, ZZ) if (b - 1) % 2 == 0 else (zz2_t, ZZ2)
                    ln_B(l, "ln3", b - 1, pz, pa, (b - 1) % 2)
                ln_A2(l, "ln3", b, zt_, za_, b % 2)

            xq(0)
            for b in range(NBLK):
                xatt(b)
                if b + 1 < NBLK:
                    xq(b + 1)
                xtail(b)
            ln_B(l, "ln3", NBLK - 1, zz2_t, ZZ2, (NBLK - 1) % 2)

        for s in range(SPC):
            load_x(s)
            for (l, st_) in stages:
                P.barrier()
                if st_ == "ffn1":
                    ffn(l, 1, "ln1")
                elif st_ == "ffn2":
                    ffn(l, 2, "ln4")
                elif st_ == "mixer":
                    mixer_consts(l)
                    mixer_phase1(l, s)
                    if "p1only" not in DBG:
                        mixer_phase2(l, s)
                elif st_ == "xattn":
                    xattn(l, s)
            P.barrier()
            store_x(s)
        P.op("sp", lambda e: e.nop(), extra=list(P.pending_dma))

        P.finalize()
        import contextlib as _c
        ses = _c.ExitStack()
        with ses:
            esems = {e: ses.enter_context(nc.semaphore(f"es_{e}")) for e in ENGS}
            dsems = [ses.enter_context(nc.semaphore(f"ds_{i}")) for i in range(P.nring)]
            P.emit(nc, esems, dsems)
    return nc


FULL_STAGES = [(l, s) for l in range(DEPTH) for s in ("ffn1", "mixer", "xattn", "ffn2")]
_CACHE = {}


def kernel(stages=None, **inp):
    if stages is None:
        stages = FULL_STAGES
    inp = {k: np.asarray(v) for k, v in inp.items()}
    x = inp["x"].astype(np.float32, copy=False)
    mem = inp["mem"].astype(np.float32, copy=False)
    pos = inp["positions"].astype(np.int32, copy=False)
    xT = np.ascontiguousarray(x.reshape(BATCH, SEQ, 8, 128).transpose(0, 3, 2, 1))
    memT = np.ascontiguousarray(mem.reshape(BATCH, MEM, 8, 128).transpose(0, 3, 2, 1))
    posb = np.ascontiguousarray(np.broadcast_to(pos[:, None, :], (BATCH, 128, SEQ)))
    wpack = np.stack([pack_layer(inp, l) for l in range(DEPTH)], axis=0)
    params = pack_params(inp)
    bc = pack_bcast(inp)
    consts = pack_consts()
    key = tuple(stages)
    if key not in _CACHE:
        _CACHE[key] = build(list(stages))
    nc = _CACHE[key]
    in_maps = []
    for c in range(NCORES):
        sl = slice(c * SPC, (c + 1) * SPC)
        in_maps.append({"xT": xT[sl], "memT": memT[sl], "posb": posb[sl], "wpack": wpack,
                        "params": params, "bcast": bc, "consts": consts})
    res = run_bass_kernel_spmd(nc, in_maps, core_ids=list(range(NCORES)))
    outs = [np.asarray(r["outT"]) for r in res.results]
    oT = np.concatenate(outs, axis=0)
    out = oT.transpose(0, 3, 2, 1).reshape(BATCH, SEQ, D)
    return np.ascontiguousarray(out.astype(np.float32, copy=False))
```
